# Optimizing a Trainium2 kernel written in Bass

```python
import math
import jax, jax.numpy as jnp
from jax import lax
import numpy as np

D_MODEL = 1024
BATCH = 16
SEQ = 4096
DEPTH = 4

GRID_W = 64
CHUNK = 128
Q_BLOCK = 128
D_A = D_MODEL // 2
H_A = 8
HD_A = D_A // H_A
D_B = D_MODEL // 2
HEAD_DIM = 64
H_B = D_B // HEAD_DIM
KV_HEADS = 2
Q_PER_KV = H_B // KV_HEADS
D_KV = KV_HEADS * HEAD_DIM
ROPE_AXIS_DIM = HEAD_DIM // 2
ROPE_THETA = 10000.0
EPS = 1e-6
SPLITS = (D_A, D_A, D_A, D_B, D_KV, D_KV, D_B)
D_IN = sum(SPLITS)
D_MIX = D_A + D_B

kernel_name = "hybrid_gmlp_axial_gqa_encoder"


def rmsnorm(x, g):
    xf = x.astype(jnp.float32)
    y = xf * lax.rsqrt(jnp.mean(xf * xf, axis=-1, keepdims=True) + EPS)
    return (y * g.astype(jnp.float32)).astype(x.dtype)


def layernorm(x, g, b):
    xf = x.astype(jnp.float32)
    mu = jnp.mean(xf, axis=-1, keepdims=True)
    var = jnp.mean(jnp.square(xf - mu), axis=-1, keepdims=True)
    y = (xf - mu) * lax.rsqrt(var + EPS)
    return (y * g.astype(jnp.float32) + b.astype(jnp.float32)).astype(x.dtype)


def axial_rope_tables(seq_len):
    rows = seq_len // GRID_W
    row = jnp.broadcast_to(jnp.arange(rows, dtype=jnp.float32)[:, None], (rows, GRID_W)).reshape(-1)
    col = jnp.broadcast_to(jnp.arange(GRID_W, dtype=jnp.float32)[None, :], (rows, GRID_W)).reshape(-1)
    inv = ROPE_THETA ** (-jnp.arange(0, ROPE_AXIS_DIM, 2, dtype=jnp.float32) / ROPE_AXIS_DIM)
    ang_r = row[:, None] * inv[None, :]
    ang_c = col[:, None] * inv[None, :]
    return jnp.cos(ang_r), jnp.sin(ang_r), jnp.cos(ang_c), jnp.sin(ang_c)


def _rot_half(x, cos, sin):
    n = x.shape[-1] // 2
    c = cos[:, None, :].astype(x.dtype)
    s = sin[:, None, :].astype(x.dtype)
    x1, x2 = x[..., :n], x[..., n:]
    return jnp.concatenate([x1 * c - x2 * s, x2 * c + x1 * s], axis=-1)


def apply_axial_rope(x, tables):
    cr, sr, cc, sc = tables
    xr = _rot_half(x[..., :ROPE_AXIS_DIM], cr, sr)
    xc = _rot_half(x[..., ROPE_AXIS_DIM:], cc, sc)
    return jnp.concatenate([xr, xc], axis=-1)


def gmlp_group(u, v, ln_g, ln_b, ws, sb):
    b, s, _ = v.shape
    nc = s // CHUNK
    vn = layernorm(v, ln_g, ln_b).reshape(b, nc, CHUNK, H_A, HD_A)
    z = jnp.einsum('hij,bcjhd->bcihd', ws, vn) + sb.T[None, None, :, :, None]
    return u * z.reshape(b, s, D_A)


def gqa_axial_attention(q, k, v, qn_g, kn_g, tables):
    b, s, _ = q.shape
    q = rmsnorm(q.reshape(b, s, H_B, HEAD_DIM), qn_g)
    k = rmsnorm(k.reshape(b, s, KV_HEADS, HEAD_DIM), kn_g)
    v = v.reshape(b, s, KV_HEADS, HEAD_DIM)
    q = apply_axial_rope(q, tables)
    k = apply_axial_rope(k, tables)
    scale = HEAD_DIM ** -0.5
    nb = s // Q_BLOCK
    qb = q.reshape(b, nb, Q_BLOCK, KV_HEADS, Q_PER_KV, HEAD_DIM).transpose(1, 0, 2, 3, 4, 5)

    def attend(qblk):
        sc = jnp.einsum('bqkgd,bskd->bkgqs', qblk, k).astype(jnp.float32) * scale
        p = jax.nn.softmax(sc, axis=-1).astype(v.dtype)
        return jnp.einsum('bkgqs,bskd->bqkgd', p, v)

    o = lax.map(attend, qb)
    return o.transpose(1, 0, 2, 3, 4, 5).reshape(b, s, D_B)


def setup_inputs(seed: int = 0) -> dict:
    key = jax.random.key(seed)
    ks = jax.random.split(key, 12)
    f32 = jnp.float32
    x = jax.random.normal(ks[0], (BATCH, SEQ, D_MODEL), f32)
    w_in = jax.random.normal(ks[1], (DEPTH, D_MODEL, D_IN), f32) * D_MODEL ** -0.5
    w_out = jax.random.normal(ks[2], (DEPTH, D_MIX, D_MODEL), f32) * D_MIX ** -0.5
    pre_g = 1.0 + 0.05 * jax.random.normal(ks[3], (DEPTH, D_MODEL), f32)
    post_g = 1.0 + 0.05 * jax.random.normal(ks[4], (DEPTH, D_MODEL), f32)
    ln_a_g = 1.0 + 0.05 * jax.random.normal(ks[5], (DEPTH, D_A), f32)
    ln_a_b = 0.02 * jax.random.normal(ks[6], (DEPTH, D_A), f32)
    spatial_w = jax.random.normal(ks[7], (DEPTH, H_A, CHUNK, CHUNK), f32) * CHUNK ** -0.5
    spatial_b = 1.0 + 0.05 * jax.random.normal(ks[8], (DEPTH, H_A, CHUNK), f32)
    q_norm_g = 1.0 + 0.05 * jax.random.normal(ks[9], (DEPTH, HEAD_DIM), f32)
    k_norm_g = 1.0 + 0.05 * jax.random.normal(ks[10], (DEPTH, HEAD_DIM), f32)
    return {"x": x, "w_in": w_in, "w_out": w_out, "pre_g": pre_g, "post_g": post_g,
            "ln_a_g": ln_a_g, "ln_a_b": ln_a_b, "spatial_w": spatial_w, "spatial_b": spatial_b,
            "q_norm_g": q_norm_g, "k_norm_g": k_norm_g}


def reference(x, w_in, w_out, pre_g, post_g, ln_a_g, ln_a_b, spatial_w, spatial_b, q_norm_g, k_norm_g):
    seq_len = x.shape[1]
    tables = axial_rope_tables(seq_len)
    idx = list(np.cumsum(SPLITS)[:-1])
    for l in range(DEPTH):
        h = rmsnorm(x, pre_g[l])
        proj = jnp.einsum('bsd,de->bse', h, w_in[l])
        u_a, v_a, g_a, q, k, v, g_b = jnp.split(proj, idx, axis=-1)
        y_a = gmlp_group(u_a, v_a, ln_a_g[l], ln_a_b[l], spatial_w[l], spatial_b[l]) * jax.nn.silu(g_a)
        y_b = gqa_axial_attention(q, k, v, q_norm_g[l], k_norm_g[l], tables) * jax.nn.silu(g_b)
        y = jnp.concatenate([y_a, y_b], axis=-1)
        out = jnp.einsum('bse,ed->bsd', y, w_out[l])
        x = x + rmsnorm(out, post_g[l])
    return x
```

```python
import contextlib
import numpy as np
import concourse.bass as bass
import concourse.mybir as mybir
from concourse.bass_utils import run_bass_kernel_spmd

F32 = mybir.dt.float32
BF16 = mybir.dt.bfloat16
AF = mybir.ActivationFunctionType
ALU = mybir.AluOpType
AX = mybir.AxisListType

D = 1024
S = 4096
NB = S // 128
DEPTH = 4
DIN = 2816
NSEQ = 2
EPS = 1e-6
OU, OVA, OGA, OQ, OK_, OV, OGB = 0, 512, 1024, 1536, 2048, 2176, 2304

N_LAYERS_PER_LAUNCH = 4


class Buf:
    __slots__ = ("name", "w", "r", "psum")

    def __init__(self, name, psum=False):
        self.name = name
        self.w = None
        self.r = {}
        self.psum = psum


class Op:
    __slots__ = ("eng", "fn", "deps", "sem", "val", "ndma", "has_dep", "dma_sem")


class Prog:
    ENGS = ("sp", "act", "pool", "dve", "pe")

    def __init__(self):
        self.q = {e: [] for e in self.ENGS}

    def op(self, eng, fn, reads=(), writes=(), dma_sem=None, ndma=0):
        o = Op()
        o.eng = eng
        o.fn = fn
        o.dma_sem = dma_sem
        o.ndma = ndma
        o.has_dep = False
        o.sem = None
        o.val = 0
        deps = set()
        for b in reads:
            if b.w is not None:
                deps.add(b.w)
            if b.psum:
                for k, r in b.r.items():
                    if k != eng:
                        deps.add(r)
        for b in writes:
            if b.w is not None:
                deps.add(b.w)
            for r in b.r.values():
                deps.add(r)
        if eng == "pe":
            deps = {d for d in deps if d.eng != "pe"}
        o.deps = deps
        for b in writes:
            b.w = o
            b.r = {}
        key = ("dma", id(dma_sem)) if dma_sem is not None else eng
        for b in reads:
            b.r[key] = o
        self.q[eng].append(o)
        return o

    def finalize(self, eng_sems):
        for e in self.ENGS:
            for o in self.q[e]:
                for d in o.deps:
                    d.has_dep = True
        dma_cnt = {}
        for e in self.ENGS:
            cnt = 0
            for o in self.q[e]:
                if o.dma_sem is not None:
                    k = id(o.dma_sem)
                    dma_cnt[k] = dma_cnt.get(k, 0) + 16 * o.ndma
                    o.sem = o.dma_sem
                    o.val = dma_cnt[k]
                elif o.has_dep:
                    cnt += 1
                    o.sem = eng_sems[e]
                    o.val = cnt

    def emit(self, eng, e, eng_sems, final_waits=()):
        waited = {}
        for o in self.q[eng]:
            need = {}
            for d in o.deps:
                k = id(d.sem)
                if waited.get(k, 0) < d.val and need.get(k, (None, 0))[1] < d.val:
                    need[k] = (d.sem, d.val)
            for k, (sem, v) in need.items():
                e.wait_ge(sem, v)
                waited[k] = v
            ins = o.fn(e)
            if o.dma_sem is None and o.has_dep:
                ins.then_inc(eng_sems[eng], 1)
        for sem, v in final_waits:
            e.wait_ge(sem, v)


def build_nc(n_layers):
    nc = bass.Bass("TRN2", target_bir_lowering=False)
    L = n_layers
    x_d = nc.dram_tensor("x", [NSEQ, S, D], F32, kind="ExternalInput").ap()
    out_d = nc.dram_tensor("out", [NSEQ, S, D], F32, kind="ExternalOutput").ap()
    win_d = nc.dram_tensor("w_in", [L, D, DIN], F32, kind="ExternalInput").ap()
    wout_d = nc.dram_tensor("w_out_p", [L, D, D], F32, kind="ExternalInput").ap()
    preg_d = nc.dram_tensor("pre_g_t", [L, 128, 8], F32, kind="ExternalInput").ap()
    postg_d = nc.dram_tensor("post_g", [L, D], F32, kind="ExternalInput").ap()
    lng_d = nc.dram_tensor("ln_g", [L, 512], F32, kind="ExternalInput").ap()
    lnb_d = nc.dram_tensor("ln_b", [L, 512], F32, kind="ExternalInput").ap()
    wst_d = nc.dram_tensor("ws_t", [L, 128, 1024], F32, kind="ExternalInput").ap()
    sbt_d = nc.dram_tensor("sb_t", [L, 128, 8], F32, kind="ExternalInput").ap()
    qg_d = nc.dram_tensor("qg", [L, 64], F32, kind="ExternalInput").ap()
    kg_d = nc.dram_tensor("kg", [L, 64], F32, kind="ExternalInput").ap()
    ropc_d = nc.dram_tensor("rope_c", [128, NB * 64], F32, kind="ExternalInput").ap()
    rops_d = nc.dram_tensor("rope_s", [128, NB * 64], F32, kind="ExternalInput").ap()
    ident_d = nc.dram_tensor("ident", [128, 128], F32, kind="ExternalInput").ap()

    P = Prog()
    es = contextlib.ExitStack()
    with es:
        def sb(name, shape, dt=F32):
            return es.enter_context(nc.sbuf_tensor(name, shape, dt))

        def sem(name):
            return es.enter_context(nc.semaphore(name))

        WIN = sb("WIN", [128, 8, DIN], BF16)
        WOUT = sb("WOUT", [128, 8, D], BF16)
        STG = [sb(f"STG{i}", [128, 2560], F32) for i in range(2)]
        KT = sb("KT", [128, S], BF16)
        VX = sb("VX", [128, NB, 192], BF16)
        ROPC = sb("ROPC", [128, NB, 64], F32)
        ROPS = sb("ROPS", [128, NB, 64], F32)
        IDB = sb("IDB", [128, 128], BF16)
        PREG = sb("PREG", [128, 8], F32)
        POSTG = sb("POSTG", [128, D], F32)
        LNG = sb("LNG", [128, 512], F32)
        LNB = sb("LNB", [128, 512], F32)
        QG = sb("QG", [128, 64], F32)
        KG = sb("KG", [128, 64], F32)
        SBT = sb("SBT", [128, 8], F32)
        WST = sb("WST", [128, 8, 128], BF16)
        NEGH = sb("NEGH", [128, 16], F32)
        RALL = sb("RALL", [128, NB], F32)
        XT = [sb(f"XT{i}", [128, D], F32) for i in range(3)]
        XB = [sb(f"XB{i}", [128, D], BF16) for i in range(2)]
        HT = [sb(f"HT{i}", [128, 8, 128], BF16) for i in range(2)]
        ST = [sb(f"ST{i}", [128, 64], F32) for i in range(2)]
        VA = sb("VA", [128, 512], F32)
        GA = sb("GA", [128, 512], F32)
        UU = sb("UU", [128, 512], F32)
        QS = sb("QS", [128, 512], F32)
        GB = sb("GB", [128, 512], F32)
        EG = sb("EG", [128, 512], F32)
        ZB = sb("ZB", [128, 512], F32)
        T1 = sb("T1", [128, 512], F32)
        T2 = sb("T2", [128, 512], F32)
        VN = sb("VN", [128, 512], BF16)
        YA = sb("YA", [128, 512], BF16)
        QP = sb("QP", [128, 512], BF16)
        GBS = sb("GBS", [128, 512], BF16)
        QT = [sb(f"QT{i}", [128, 512], BF16) for i in range(2)]
        GT = [sb(f"GT{i}", [128, 512], BF16) for i in range(2)]
        NE = 3
        ET = [sb(f"ET{i}", [128, 1024], BF16) for i in range(NE)]
        RD = sb("RD", [128, 512], F32)
        A0 = sb("A0", [128, 512], F32)
        A1 = sb("A1", [128, 512], F32)
        TT = sb("TT", [128, 512], F32)
        YT = [sb(f"YT{i}", [128, 8, 128], BF16) for i in range(2)]
        TMP = sb("TMP", [128, D], F32)
        XO = [sb("XO0", [128, D], F32)]
        JUNK = sb("JUNK", [128, D], BF16)
        KS2 = sb("KS2", [128, 256], F32)
        KT1w = sb("KT1w", [128, 256], F32)
        KT2w = sb("KT2w", [128, 256], F32)
        KRw = sb("KRw", [128, 256], BF16)

        SPS = es.enter_context(nc.psum_tensor("SPS", [128, 2048], F32))
        PO = [es.enter_context(nc.psum_tensor(f"PO{i}", [128, 512], F32)) for i in range(2)]
        PY = es.enter_context(nc.psum_tensor("PY", [128, 512], F32))
        PTR = es.enter_context(nc.psum_tensor("PTR", [128, 1024], BF16))
        PTRF = PTR[:].bitcast(F32)

        eng_sems = {e: sem("s_" + e) for e in Prog.ENGS}
        sm_const = sem("d_const")
        sm_lw = sem("d_lw")
        sm_stg = [sem(f"d_stg{i}") for i in range(2)]
        sm_x = [sem(f"d_x{i}") for i in range(3)]
        sm_xo = [sem(f"d_xo{i}") for i in range(2)]

        def B(name):
            return Buf(name)
        bWIN = [B(f"WIN{k}") for k in range(8)]
        bWOUT = [B(f"WOUT{k}") for k in range(8)]
        bSTG = [B("STG0"), B("STG1")]
        bKT = [B(f"KT{b}") for b in range(NB)]
        bVX = [B(f"VX{b}") for b in range(NB)]
        bCONST = B("CONST")
        bLW = B("LW")
        bWST = B("WST")
        bXT = [B("XT0"), B("XT1"), B("XT2")]
        bXB = [B("XB0"), B("XB1")]
        bHT = [B("HT0"), B("HT1")]
        bST = [[B(f"ST{i}_{j}") for j in range(16)] for i in range(2)]
        names = ["VA", "GA", "UU", "QS", "GB", "EG", "ZB", "T1", "T2", "VN", "YA", "QP", "GBS", "RD", "TT", "TMP", "A0", "A1"]
        bb = {n: B(n) for n in names}
        bKS2, bKT1w, bKT2w, bKRw = B("KS2"), B("KT1w"), B("KT2w"), B("KRw")
        bQT = [B("QT0"), B("QT1")]
        bGT = [B("GT0"), B("GT1")]
        bET = [B(f"ET{i}") for i in range(NE)]
        bYT = [[B(f"YT{i}A"), B(f"YT{i}B")] for i in range(2)]
        bXO = [B("XO0"), B("XO1")]
        bS = [Buf("S0", True), Buf("S1", True)]
        bPO = [Buf("PO0", True), Buf("PO1", True)]
        bY = Buf("PY", True)
        bPTR = Buf("PTR", True)
        bXD = [[B(f"XD{s}_{b}") for b in range(NB)] for s in range(NSEQ)]

        C_SS, C_LN, C_RSTD, C_NRSTD, C_BN, C_MV, C_LNV, C_RSTDV = 0, 1, 2, 3, 4, 10, 12, 13
        C_SSQ, C_LNQ, C_RQ, C_SS2, C_LN2, C_RSTD2 = 16, 24, 32, 40, 42, 43
        def act(fn, reads, writes):
            return P.op("act", fn, reads, writes)

        def dve(fn, reads, writes):
            return P.op("dve", fn, reads, writes)

        def pool(fn, reads, writes):
            return P.op("pool", fn, reads, writes)

        def pe(fn, reads, writes):
            return P.op("pe", fn, reads, writes)

        def dma(eng, sem_h, pairs, reads, writes, **kw):
            def fn(e, pairs=pairs):
                ins = None
                for (o, i) in pairs:
                    ins = e.dma_start(out=o, in_=i, **kw)
                    ins.then_inc(sem_h, 16)
                return ins
            return P.op(eng, fn, reads, writes, dma_sem=sem_h, ndma=len(pairs))

        def rsqrt_chain(slot, c_in, c_ln, c_out, n, scale, bufs_in, buf_out):
            st = ST[slot]
            tmpb = Buf("lnchain")
            pool(lambda e: e.tensor_scalar(out=st[:, c_ln:c_ln + n], in0=st[:, c_in:c_in + n], scalar1=scale,
                                           scalar2=EPS, op0=ALU.mult, op1=ALU.add), bufs_in, [tmpb])
            pool(lambda e: e.tensor_tensor(out=st[:, c_out:c_out + n], in0=st[:, c_ln:c_ln + n], in1=NEGH[:, 0:n],
                                           op=ALU.pow), [tmpb, bCONST], [buf_out])

        dma("sp", sm_const, [(ROPC[:].rearrange("p b d -> p (b d)"), ropc_d),
                             (ROPS[:].rearrange("p b d -> p (b d)"), rops_d),
                             (STG[0][:, 0:128], ident_d)], [], [bCONST, bSTG[0]])
        pool(lambda e: e.memset(NEGH[:, 0:8], -0.5), [], [bCONST])
        pool(lambda e: e.memset(NEGH[:, 8:16], -1.0), [], [bCONST])
        pool(lambda e: e.memset(VX[:].rearrange("p b c -> p (b c)"), 1.0), [], bVX)
        dve(lambda e: e.tensor_copy(out=IDB[:], in_=STG[0][:, 0:128]), [bCONST, bSTG[0]], [bCONST])

        xo_count = [0, 0]
        blk_counter = [0]

        bWKV = B("WKV")
        bRALL = [B(f"RALL{b}") for b in range(NB)]
        bJUNK = B("JUNK")

        def load_layer_small(l):
            dma("sp", sm_lw, [(PREG[:], preg_d[l]),
                              (POSTG[:], postg_d[l:l + 1, :].partition_broadcast(128)),
                              (LNG[:], lng_d[l:l + 1, :].partition_broadcast(128)),
                              (LNB[:], lnb_d[l:l + 1, :].partition_broadcast(128)),
                              (QG[:], qg_d[l:l + 1, :].partition_broadcast(128)),
                              (KG[:], kg_d[l:l + 1, :].partition_broadcast(128)),
                              (SBT[:], sbt_d[l])], [], [bLW])
            dma("sp", sm_stg[1], [(STG[1][:, kc * 256:(kc + 1) * 256], win_d[l, kc * 128:(kc + 1) * 128, OK_:OK_ + 256])
                                  for kc in range(8)], [], [bSTG[1]])
            dve(lambda e: e.tensor_tensor(out=WIN[:, :, OK_:OK_ + 256],
                                          in0=STG[1][:, 0:2048].rearrange("p (k c) -> p k c", k=8),
                                          in1=PREG[:].unsqueeze(2).to_broadcast([128, 8, 256]), op=ALU.mult),
                [bSTG[1], bLW], [bWKV])
            dma("sp", sm_stg[0], [(STG[0][:, 0:1024], wst_d[l])], [], [bSTG[0]])
            dve(lambda e: e.tensor_copy(out=WST[:].rearrange("p h i -> p (h i)"), in_=STG[0][:, 0:1024]),
                [bSTG[0]], [bWST])

        def load_layer_big(l):
            si = 0
            for kc in range(8):
                dma("sp", sm_stg[si], [(STG[si][:, 0:2048], win_d[l, kc * 128:(kc + 1) * 128, 0:2048]),
                                       (STG[si][:, 2048:2560], win_d[l, kc * 128:(kc + 1) * 128, OGB:OGB + 512])],
                    [], [bSTG[si]])
                if kc % 2 == 0:
                    act(lambda e, kc=kc, si=si: e.activation(out=WIN[:, kc, 0:2048], in_=STG[si][:, 0:2048], func=AF.Copy,
                                                             scale=PREG[:, kc:kc + 1]),
                        [bSTG[si], bLW], [bWIN[kc]])
                    act(lambda e, kc=kc, si=si: e.activation(out=WIN[:, kc, OGB:OGB + 512], in_=STG[si][:, 2048:2560],
                                                             func=AF.Copy, scale=PREG[:, kc:kc + 1]),
                        [bSTG[si], bLW], [bWIN[kc]])
                else:
                    dve(lambda e, kc=kc, si=si: e.tensor_scalar(out=WIN[:, kc, 0:2048], in0=STG[si][:, 0:2048],
                                                                scalar1=PREG[:, kc:kc + 1], scalar2=None,
                                                                op0=ALU.mult),
                        [bSTG[si], bLW], [bWIN[kc]])
                    dve(lambda e, kc=kc, si=si: e.tensor_scalar(out=WIN[:, kc, OGB:OGB + 512], in0=STG[si][:, 2048:2560],
                                                                scalar1=PREG[:, kc:kc + 1], scalar2=None,
                                                                op0=ALU.mult),
                        [bSTG[si], bLW], [bWIN[kc]])
                si ^= 1
                yield
            for kc in range(8):
                dma("sp", sm_stg[si], [(STG[si][:, 0:D], wout_d[l, kc * 128:(kc + 1) * 128, :])], [], [bSTG[si]])
                if kc % 2 == 0:
                    act(lambda e, kc=kc, si=si: e.activation(out=WOUT[:, kc, :], in_=STG[si][:, 0:D], func=AF.Copy),
                        [bSTG[si]], [bWOUT[kc]])
                else:
                    dve(lambda e, kc=kc, si=si: e.tensor_copy(out=WOUT[:, kc, :], in_=STG[si][:, 0:D]),
                        [bSTG[si]], [bWOUT[kc]])
                si ^= 1
                yield

        def block_front(src_d, s, b, slot):
            xt, xb, ht, st = XT[slot], XB[slot], HT[slot], ST[slot]
            dma("sp", sm_x[slot], [(xt[:], src_d[s, b * 128:(b + 1) * 128, :])], [bXD[s][b]], [bXT[slot]])
            act(lambda e: e.activation(out=xb[:], in_=xt[:], func=AF.Copy), [bXT[slot]], [bXB[slot]])
            act(lambda e: e.activation(out=JUNK[:], in_=xt[:], func=AF.Square, accum_out=st[:, C_SS:C_SS + 1]),
                [bXT[slot]], [bST[slot][0], bJUNK])
            rsqrt_chain(slot, C_SS, C_LN, C_RSTD, 1, 1.0 / D, [bST[slot][0]], bST[slot][1])

            def tr(e):
                ins = None
                for kc in range(8):
                    ins = e.transpose(out=PTR[:, kc * 128:(kc + 1) * 128], in_=xb[:, kc * 128:(kc + 1) * 128],
                                      identity=IDB[:])
                return ins
            pe(tr, [bXB[slot], bCONST], [bPTR])
            act(lambda e: e.activation(out=ht[:].rearrange("p k t -> p (k t)"), in_=PTR[:], func=AF.Copy), [bPTR], [bHT[slot]])

        def proj_part(co, n, slot, k0, k1, alt=False):
            ht = HT[slot]
            bank, bbank = (PTRF, bPTR) if alt else (PY, bY)

            def fn(e):
                ins = None
                for kc in range(k0, k1):
                    ins = e.matmul(bank[:, 0:n], lhsT=ht[:, kc, :], rhs=WIN[:, kc, co:co + n],
                                   start=(kc == 0), stop=(kc == 7))
                return ins
            pe(fn, [bHT[slot]] + ([bWKV] if co == OK_ else bWIN[k0:k1]), [bbank])

        def pview(t, g):
            return t[:, 0:512].rearrange("p (j g d) -> p j g d", j=4, g=2, d=64)[:, :, g, :]

        def nview(t, g):
            return t[:, g * 256:(g + 1) * 256].rearrange("p (j d) -> p j d", j=4)

        def rope(src, t1, t2, dst_view, H, b, bsrc, bt1, bt2, bdst, vw=None):
            cb = ROPC[:, b, :].unsqueeze(1).to_broadcast([128, H, 64])
            srs = ROPS[:, b, :].rearrange("p (a t i) -> p a t i", a=2, t=2, i=16)
            s3 = src[:, 0:H * 64].rearrange("p (h d) -> p h d", h=H)
            s5 = src[:, 0:H * 64].rearrange("p (h a t i) -> p h a t i", h=H, a=2, t=2, i=16)
            t15 = t1[:, 0:H * 64].rearrange("p (h d) -> p h d", h=H)
            t25 = t2[:, 0:H * 64].rearrange("p (h a t i) -> p h a t i", h=H, a=2, t=2, i=16)
            dve(lambda e: e.tensor_tensor(out=t15, in0=s3, in1=cb, op=ALU.mult), [bsrc, bCONST], [bt1])
            for t in range(2):
                sv = srs[:, :, t, :].unsqueeze(1).to_broadcast([128, H, 2, 16])
                dve(lambda e, t=t, sv=sv: e.tensor_tensor(out=t25[:, :, :, t, :], in0=s5[:, :, :, 1 - t, :], in1=sv,
                                                          op=ALU.mult), [bsrc, bCONST], [bt2])
            if vw is None:
                dve(lambda e: e.tensor_tensor(out=dst_view, in0=t1[:, 0:H * 64], in1=t2[:, 0:H * 64], op=ALU.add),
                    [bt1, bt2], [bdst])
            else:
                for g in range(2):
                    dve(lambda e, g=g: e.tensor_tensor(out=pview(dst_view, g), in0=nview(t1, g), in1=nview(t2, g),
                                                       op=ALU.add), [bt1, bt2], [bdst])

        P_SS, P_LN, P_RSTD = 44, 46, 48

        def phase1_seq(src_d, s, wgen=None):
            npair = NB // 2
            xsl = lambda i, t: (2 * i + t) % 3

            def loads(i):
                for t in range(2):
                    b = 2 * i + t
                    x3 = xsl(i, t)
                    dma("sp", sm_x[x3], [(XT[x3][:], src_d[s, b * 128:(b + 1) * 128, :])], [bXD[s][b]], [bXT[x3]])
            def do_pair(i):
                b0 = 2 * i
                sl = i % 2
                st = ST[sl]
                bs = bST[sl]
                for t in range(2):
                    x3 = xsl(i, t)
                    dve(lambda e, t=t, x3=x3: e.tensor_copy(out=XB[t][:], in_=XT[x3][:]), [bXT[x3]], [bXB[t]])
                    act(lambda e, t=t, x3=x3: e.activation(out=JUNK[:], in_=XT[x3][:], func=AF.Square,
                                                           accum_out=st[:, P_SS + t:P_SS + t + 1]),
                        [bXT[x3]], [bs[0], bJUNK])
                if i + 1 < npair:
                    loads(i + 1)
                rsqrt_chain(sl, P_SS, P_LN, P_RSTD, 2, 1.0 / D, [bs[0]], bs[1])
                pool(lambda e: e.tensor_copy(out=RALL[:, b0:b0 + 2], in_=st[:, P_RSTD:P_RSTD + 2]), [bs[1]],
                     [bRALL[b0], bRALL[b0 + 1]])
                for t in range(2):
                    def tr(e, t=t):
                        ins = None
                        for kc in range(8):
                            ins = e.transpose(out=PTR[:, kc * 128:(kc + 1) * 128], in_=XB[t][:, kc * 128:(kc + 1) * 128],
                                              identity=IDB[:])
                        return ins
                    pe(tr, [bXB[t], bCONST], [bPTR])
                    act(lambda e, t=t: e.activation(out=HT[t][:].rearrange("p k t -> p (k t)"), in_=PTR[:], func=AF.Copy),
                        [bPTR], [bHT[t]])
                for t in range(2):
                    def pj(e, t=t):
                        ins = None
                        for kc in range(8):
                            ins = e.matmul(PY[:, t * 256:(t + 1) * 256], lhsT=HT[t][:, kc, :], rhs=WIN[:, kc, OK_:OK_ + 256],
                                           start=(kc == 0), stop=(kc == 7))
                        return ins
                    pe(pj, [bHT[t], bWKV], [bY])
                rs2 = st[:, P_RSTD:P_RSTD + 2]
                pyv = PY[:].rearrange("p (t c) -> p t c", t=2)
                dve(lambda e: e.tensor_tensor(out=KS2[:].rearrange("p (t c) -> p t c", t=2), in0=pyv[:, :, 0:128],
                                              in1=rs2.unsqueeze(2).to_broadcast([128, 2, 128]), op=ALU.mult),
                    [bY, bs[1]], [bKS2])
                for t in range(2):
                    rst = st[:, P_RSTD + t:P_RSTD + t + 1]
                    act(lambda e, t=t, rst=rst: e.activation(out=VX[:, b0 + t, 0:64], in_=PY[:, t * 256 + 128:t * 256 + 192],
                                                             func=AF.Copy, scale=rst), [bY, bs[1]], [bVX[b0 + t]])
                    act(lambda e, t=t, rst=rst: e.activation(out=VX[:, b0 + t, 128:192], in_=PY[:, t * 256 + 192:t * 256 + 256],
                                                             func=AF.Copy, scale=rst), [bY, bs[1]], [bVX[b0 + t]])
                dve(lambda e: e.tensor_tensor(out=KT1w[:], in0=KS2[:], in1=KS2[:], op=ALU.mult), [bKS2], [bKT1w])
                dve(lambda e: e.tensor_reduce(out=st[:, C_SSQ:C_SSQ + 4], in_=KT1w[:].rearrange("p (h d) -> p h d", h=4),
                                              axis=AX.X, op=ALU.add), [bKT1w], [bs[4]])
                rsqrt_chain(sl, C_SSQ, C_LNQ, C_RQ, 4, 1.0 / 64, [bs[4]], bs[5])
                k4 = KS2[:].rearrange("p (h d) -> p h d", h=4)
                dve(lambda e: e.tensor_tensor(out=k4, in0=k4, in1=st[:, C_RQ:C_RQ + 4].unsqueeze(2).to_broadcast([128, 4, 64]),
                                              op=ALU.mult), [bKS2, bs[5]], [bKS2])
                dve(lambda e: e.tensor_tensor(out=k4, in0=k4, in1=KG[:].unsqueeze(1).to_broadcast([128, 4, 64]),
                                              op=ALU.mult), [bKS2, bLW], [bKS2])
                for t in range(2):
                    dve(lambda e, t=t: e.tensor_tensor(out=KT1w[:, t * 128:(t + 1) * 128].rearrange("p (h d) -> p h d", h=2),
                                                       in0=KS2[:, t * 128:(t + 1) * 128].rearrange("p (h d) -> p h d", h=2),
                                                       in1=ROPC[:, b0 + t, :].unsqueeze(1).to_broadcast([128, 2, 64]),
                                                       op=ALU.mult), [bKS2, bCONST], [bKT1w])
                for t in range(2):
                    srs = ROPS[:, b0 + t, :].rearrange("p (a u i) -> p a u i", a=2, u=2, i=16)
                    s5 = KS2[:, t * 128:(t + 1) * 128].rearrange("p (h a u i) -> p h a u i", h=2, a=2, u=2, i=16)
                    t25 = KT2w[:, t * 128:(t + 1) * 128].rearrange("p (h a u i) -> p h a u i", h=2, a=2, u=2, i=16)
                    for u in range(2):
                        sv = srs[:, :, u, :].unsqueeze(1).to_broadcast([128, 2, 2, 16])
                        dve(lambda e, u=u, sv=sv, s5=s5, t25=t25: e.tensor_tensor(out=t25[:, :, :, u, :],
                                                                                  in0=s5[:, :, :, 1 - u, :], in1=sv,
                                                                                  op=ALU.mult),
                            [bKS2, bCONST], [bKT2w])
                dve(lambda e: e.tensor_tensor(out=KRw[:], in0=KT1w[:], in1=KT2w[:], op=ALU.add), [bKT1w, bKT2w], [bKRw])

                def trk(e):
                    ins = None
                    for t in range(2):
                        ins = e.transpose(out=PTR[:, t * 128:(t + 1) * 128], in_=KRw[:, t * 128:(t + 1) * 128],
                                          identity=IDB[:])
                    return ins
                pe(trk, [bKRw, bCONST], [bPTR])
                dve(lambda e: e.tensor_copy(out=KT[:, b0 * 128:(b0 + 2) * 128], in_=PTR[:, 0:256]), [bPTR],
                    [bKT[b0], bKT[b0 + 1]])
                if wgen is not None:
                    next(wgen, None)
            loads(0)
            for i in range(npair):
                do_pair(i)
            if wgen is not None:
                for _ in wgen:
                    pass

        def tr4(src):
            def fn(e):
                ins = None
                for c in range(4):
                    ins = e.transpose(out=PTR[:, c * 128:(c + 1) * 128], in_=src[:, c * 128:(c + 1) * 128],
                                      identity=IDB[:])
                return ins
            return fn

        def tanh_half(src, bsrc):
            act(lambda e: e.activation(out=EG[:], in_=src[:], func=AF.Tanh, scale=0.5), [bsrc], [bb["EG"]])

        def planA(src_d, s, b, slot, x3, with_f0=True, only_f0=False):
            st = ST[slot]
            bs = bST[slot]
            yt = YT[slot]
            rstd = RALL[:, b:b + 1]
            plan = {}

            def at(c, fn):
                plan.setdefault(c, []).append(fn)

            def f0a():
                dma("sp", sm_x[x3], [(XT[x3][:], src_d[s, b * 128:(b + 1) * 128, :])], [bXD[s][b]], [bXT[x3]])

            def f0b():
                xt, xb = XT[x3], XB[slot]
                dve(lambda e: e.tensor_copy(out=xb[:], in_=xt[:]), [bXT[x3]], [bXB[slot]])

            def f0():
                f0a()
                f0b()
            if only_f0:
                return {27: [f0a], 31: [f0b]}
            if with_f0:
                at(0, f0)

            def f1a():
                xb = XB[slot]

                def tr(e):
                    ins = None
                    for kc in range(4):
                        ins = e.transpose(out=PTR[:, kc * 128:(kc + 1) * 128], in_=xb[:, kc * 128:(kc + 1) * 128],
                                          identity=IDB[:])
                    return ins
                pe(tr, [bXB[slot], bCONST], [bPTR])

            def f1b():
                xb, ht = XB[slot], HT[slot]

                def tr(e):
                    ins = None
                    for kc in range(4, 8):
                        ins = e.transpose(out=PTR[:, kc * 128:(kc + 1) * 128], in_=xb[:, kc * 128:(kc + 1) * 128],
                                          identity=IDB[:])
                    return ins
                pe(tr, [bXB[slot], bCONST], [bPTR])
                dve(lambda e: e.tensor_copy(out=ht[:].rearrange("p k t -> p (k t)"), in_=PTR[:]), [bPTR], [bHT[slot]])
            at(1, f1a)
            at(2, f1b)

            def group(c, co, dst, nm, alt=False):
                bank, bbank = (PTRF, bPTR) if alt else (PY, bY)
                for i in range(4):
                    if i < 3:
                        at(c + i, lambda i=i: proj_part(co, 512, slot, 2 * i, 2 * i + 2, alt))
                    else:
                        def last():
                            proj_part(co, 512, slot, 6, 8, alt)
                            dve(lambda e: e.tensor_scalar(out=dst[:], in0=bank[:, 0:512], scalar1=rstd, scalar2=None,
                                                          op0=ALU.mult), [bbank, bRALL[b]], [bb[nm]])
                        at(c + 3, last)
            group(3, OVA, VA, "VA")
            group(12, OQ, QS, "QS")
            group(21, OGA, GA, "GA", alt=True)
            group(24, OU, UU, "UU")
            group(28, OGB, GB, "GB", alt=True)

            def ln():
                dve(lambda e: e.bn_stats(out=st[:, C_BN:C_BN + 6], in_=VA[:]), [bb["VA"]], [bs[6]])
                dve(lambda e: e.bn_aggr(out=st[:, C_MV:C_MV + 2], in_=st[:, C_BN:C_BN + 6]), [bs[6]], [bs[7]])
                rsqrt_chain(slot, C_MV + 1, C_LNV, C_RSTDV, 1, 1.0, [bs[7]], bs[8])
                dve(lambda e: e.scalar_tensor_tensor(out=VA[:], in0=VA[:], scalar=st[:, C_MV:C_MV + 1], in1=LNG[:],
                                                     op0=ALU.subtract, op1=ALU.mult), [bb["VA"], bs[7], bLW], [bb["VA"]])
            at(7, ln)
            at(9, lambda: dve(lambda e: e.scalar_tensor_tensor(out=VN[:], in0=VA[:], scalar=st[:, C_RSTDV:C_RSTDV + 1],
                                                                in1=LNB[:], op0=ALU.mult, op1=ALU.add),
                               [bb["VA"], bs[8], bLW], [bb["VN"]]))

            def q1():
                dve(lambda e: e.tensor_tensor(out=T1[:], in0=QS[:], in1=QS[:], op=ALU.mult), [bb["QS"]], [bb["T1"]])
                dve(lambda e: e.tensor_reduce(out=st[:, C_SSQ:C_SSQ + 8], in_=T1[:].rearrange("p (h d) -> p h d", h=8),
                                              axis=AX.X, op=ALU.add), [bb["T1"]], [bs[4]])
                rsqrt_chain(slot, C_SSQ, C_LNQ, C_RQ, 8, 1.0 / 64, [bs[4]], bs[5])
            at(16, q1)

            def q2():
                dve(lambda e: e.tensor_tensor(out=QS[:].rearrange("p (h d) -> p h d", h=8),
                                              in0=QS[:].rearrange("p (h d) -> p h d", h=8),
                                              in1=st[:, C_RQ:C_RQ + 8].unsqueeze(2).to_broadcast([128, 8, 64]),
                                              op=ALU.mult), [bb["QS"], bs[5]], [bb["QS"]])
                dve(lambda e: e.tensor_tensor(out=QS[:].rearrange("p (h d) -> p h d", h=8),
                                              in0=QS[:].rearrange("p (h d) -> p h d", h=8),
                                              in1=QG[:].unsqueeze(1).to_broadcast([128, 8, 64]),
                                              op=ALU.mult), [bb["QS"], bLW], [bb["QS"]])
            at(18, q2)
            at(21, lambda: rope(QS, T1, T2, QP, 8, b, bb["QS"], bb["T1"], bb["T2"], bb["QP"], vw=True))

            def zz():
                def zmm(e):
                    ins = None
                    for h in range(8):
                        ins = e.matmul(PY[:, h * 64:(h + 1) * 64], lhsT=WST[:, h, :], rhs=VN[:, h * 64:(h + 1) * 64],
                                       start=True, stop=True)
                    return ins
                pe(zmm, [bb["VN"], bWST], [bY])
                dve(lambda e: e.tensor_tensor(out=ZB[:].rearrange("p (h d) -> p h d", h=8),
                                              in0=PY[:].rearrange("p (h d) -> p h d", h=8),
                                              in1=SBT[:].unsqueeze(2).to_broadcast([128, 8, 64]), op=ALU.add),
                    [bY, bLW], [bb["ZB"]])
            at(20, zz)

            def trq():
                pe(tr4(QP), [bb["QP"], bCONST], [bPTR])
                dve(lambda e: e.tensor_copy(out=QT[slot][:], in_=PTR[:, 0:512]), [bPTR], [bQT[slot]])
            at(26, trq)
            at(26, lambda: tanh_half(GA, bb["GA"]))

            def gate_a():
                dve(lambda e: e.scalar_tensor_tensor(out=GA[:], in0=EG[:], scalar=1.0, in1=GA[:], op0=ALU.add,
                                                     op1=ALU.mult), [bb["GA"], bb["EG"]], [bb["GA"]])
                dve(lambda e: e.scalar_tensor_tensor(out=UU[:], in0=UU[:], scalar=0.5, in1=GA[:], op0=ALU.mult,
                                                     op1=ALU.mult), [bb["UU"], bb["GA"]], [bb["UU"]])
                dve(lambda e: e.tensor_tensor(out=YA[:], in0=UU[:], in1=ZB[:], op=ALU.mult), [bb["UU"], bb["ZB"]],
                    [bb["YA"]])
            at(29, gate_a)
            at(33, lambda: tanh_half(GB, bb["GB"]))

            def gate_b():
                for g in range(2):
                    dve(lambda e, g=g: e.scalar_tensor_tensor(out=pview(GBS, g), in0=nview(EG, g), scalar=1.0,
                                                              in1=nview(GB, g), op0=ALU.add, op1=ALU.mult),
                        [bb["GB"], bb["EG"]], [bb["GBS"]])
            at(36, gate_b)

            def tra():
                pe(tr4(YA), [bb["YA"], bCONST], [bPTR])
                dve(lambda e: e.tensor_copy(out=yt[:, 0:4, :].rearrange("p k t -> p (k t)"), in_=PTR[:, 0:512]),
                    [bPTR], [bYT[slot][0]])
            at(35, tra)

            def trg():
                pe(tr4(GBS), [bb["GBS"], bCONST], [bPTR])
                dve(lambda e: e.tensor_copy(out=GT[slot][:], in_=PTR[:, 0:512]), [bPTR], [bGT[slot]])
            at(39, trg)
            return plan

        def planC(s, b, slot, x3):
            st = ST[slot]
            bs = bST[slot]
            yt = YT[slot]
            plan = {}

            def at(c, fn):
                plan.setdefault(c, []).append(fn)

            def op_part(n, k0, k1, evac):
                def fn():
                    def oproj(e):
                        ins = None
                        for kc in range(k0, k1):
                            ins = e.matmul(PTRF[:, 0:512], lhsT=yt[:, kc, :], rhs=WOUT[:, kc, n * 512:(n + 1) * 512],
                                           start=(kc == 0), stop=(kc == 7))
                        return ins
                    pe(oproj, [bYT[slot][0], bYT[slot][1]] + bWOUT[k0:k1], [bPTR])
                    if evac:
                        dve(lambda e: e.tensor_copy(out=TMP[:, n * 512:(n + 1) * 512], in_=PTRF[:, 0:512]), [bPTR],
                            [bb["TMP"]])
                return fn
            for n, c0 in ((0, 10), (1, 16)):
                for i in range(4):
                    at(c0 + i, op_part(n, 2 * i, 2 * i + 2, i == 3))
            xs = 0
            at(21, lambda: act(lambda e: e.activation(out=JUNK[:], in_=TMP[:], func=AF.Square,
                                                      accum_out=st[:, C_SS2:C_SS2 + 1]), [bb["TMP"]], [bs[11], bJUNK]))
            at(23, lambda: rsqrt_chain(slot, C_SS2, C_LN2, C_RSTD2, 1, 1.0 / D, [bs[11]], bs[12]))

            def fin():
                dve(lambda e: e.scalar_tensor_tensor(out=XO[xs][:], in0=TMP[:], scalar=st[:, C_RSTD2:C_RSTD2 + 1],
                                                     in1=POSTG[:], op0=ALU.mult, op1=ALU.mult),
                    [bb["TMP"], bs[12], bLW], [bXO[xs]])
                dve(lambda e: e.tensor_tensor(out=XO[xs][:], in0=XO[xs][:], in1=XT[x3][:], op=ALU.add),
                    [bXO[xs], bXT[x3]], [bXO[xs]])
            at(25, fin)
            at(26, lambda: dma("pool", sm_xo[xs], [(out_d[s, b * 128:(b + 1) * 128, :], XO[xs][:])], [bXO[xs]],
                               [bXD[s][b]]))
            return plan

        def run_plan(plans, c):
            for p in plans:
                for fn in p.get(c, ()):
                    fn()

        def attention(b, slot, plans):
            qt = QT[slot]

            def qk(c):
                par = c % 2

                def fn(e):
                    ins = None
                    for g in range(2):
                        ins = e.matmul(SPS[:, par * 1024 + g * 512: par * 1024 + (g + 1) * 512],
                                       lhsT=KT[g * 64:(g + 1) * 64, c * 128:(c + 1) * 128],
                                       rhs=qt[g * 64:(g + 1) * 64, :], start=True, stop=True,
                                       tile_position=(g * 64, 0))
                    return ins
                pe(fn, [bKT[c], bQT[slot]], [bS[par]])
            qk(0)
            for c in range(NB):
                par = c % 2
                k = c % NE
                act(lambda e, k=k, par=par: e.activation(out=ET[k][:], in_=SPS[:, par * 1024:(par + 1) * 1024],
                                                         func=AF.Exp, scale=0.125), [bS[par]], [bET[k]])
                if c + 1 < NB:
                    qk(c + 1)

                def pv(e, c=c, k=k):
                    ins = None
                    for g in range(2):
                        ins = e.matmul(PO[g][:], lhsT=VX[:, c, g * 64:g * 64 + 128], rhs=ET[k][:, g * 512:(g + 1) * 512],
                                       start=(c == 0), stop=(c == NB - 1))
                    return ins
                run_plan(plans, c)
                pe(pv, [bVX[c], bET[k]], [bPO[0], bPO[1]])

        def stageC0(slot):
            dve(lambda e: e.tensor_copy(out=A0[:], in_=PO[0][:]), [bPO[0]], [bb["A0"]])
            dve(lambda e: e.tensor_copy(out=A1[:], in_=PO[1][:]), [bPO[1]], [bb["A1"]])

        def planC0(slot):
            yt = YT[slot]
            plan = {}

            def at(c, fn):
                plan.setdefault(c, []).append(fn)

            def asm():
                dve(lambda e: e.tensor_copy(out=RD[0:64, :], in_=A0[64:128, :]), [bb["A0"]], [bb["RD"]])
                dve(lambda e: e.tensor_copy(out=RD[64:128, :], in_=A1[0:64, :]), [bb["A1"]], [bb["RD"]])
            at(3, asm)

            def rec(j):
                return lambda: dve(lambda e: e.reciprocal(out=RD[:, j * 128:(j + 1) * 128], in_=RD[:, j * 128:(j + 1) * 128]),
                                   [bb["RD"]], [bb["RD"]])
            at(4, rec(0))
            at(4, rec(1))
            at(5, rec(2))
            at(5, rec(3))

            def fin0():
                dve(lambda e: e.tensor_tensor(out=TT[0:64, :], in0=A0[0:64, :], in1=RD[0:64, :], op=ALU.mult),
                    [bb["A0"], bb["RD"]], [bb["TT"]])
                dve(lambda e: e.tensor_tensor(out=TT[64:128, :], in0=A1[64:128, :], in1=RD[64:128, :], op=ALU.mult),
                    [bb["A1"], bb["RD"]], [bb["TT"]])
                dve(lambda e: e.scalar_tensor_tensor(out=yt[:, 4:8, :].rearrange("p k t -> p (k t)"), in0=TT[:],
                                                     scalar=0.5, in1=GT[slot][:], op0=ALU.mult, op1=ALU.mult),
                    [bb["TT"], bGT[slot]], [bYT[slot][1]])
            at(6, fin0)
            return plan

        def split_plan(p):
            now = {c: f for c, f in p.items() if c < NB}
            later = {c - NB: f for c, f in p.items() if c >= NB}
            return now, later

        def phase2_seq(src_d, s):
            slots = []
            for b in range(NB):
                slots.append(blk_counter[0] % 2)
                blk_counter[0] += 1
            p0 = planA(src_d, s, 0, slots[0], 0)
            for c in sorted(p0):
                run_plan([p0], c)
            prevC = None
            carry = None
            for b in range(NB):
                plans = []
                if prevC is not None:
                    plans.append(prevC0)
                    plans.append(prevC)
                if carry:
                    plans.append(carry)
                carry = None
                if b + 1 < NB:
                    now, carry = split_plan(planA(src_d, s, b + 1, slots[b + 1], (b + 1) % 3, with_f0=(b == 0)))
                    plans.append(now)
                if b + 2 < NB:
                    plans.append(planA(src_d, s, b + 2, slots[b + 2], (b + 2) % 3, only_f0=True))
                attention(b, slots[b], plans)
                stageC0(slots[b])
                prevC0 = planC0(slots[b])
                prevC = planC(s, b, slots[b], b % 3)
            for c in sorted(set(prevC) | set(prevC0)):
                run_plan([prevC0, prevC], c)

        import os
        stage = int(os.environ.get("K_STAGE", "99"))
        for l in range(L):
            src = x_d if l == 0 else out_d
            load_layer_small(l)
            wgen = load_layer_big(l)
            if stage == 0:
                for _ in wgen:
                    pass
                continue
            for s in range(NSEQ):
                phase1_seq(src, s, wgen if s == 0 else None)
                if stage == 1:
                    break
                phase2_seq(src, s)
                if stage == 2:
                    break

        P.finalize(eng_sems)
        last_xo = {}
        for o in P.q["pool"]:
            if o.dma_sem is not None:
                last_xo[id(o.dma_sem)] = (o.dma_sem, o.val)
        finals = list(last_xo.values())

        block = es.enter_context(nc.Block())

        @block.sync
        def _(e):
            P.emit("sp", e, eng_sems)

        @block.scalar
        def _(e):
            P.emit("act", e, eng_sems)

        @block.vector
        def _(e):
            P.emit("dve", e, eng_sems)

        @block.tensor
        def _(e):
            P.emit("pe", e, eng_sems)

        @block.gpsimd
        def _(e):
            P.emit("pool", e, eng_sems, final_waits=finals)
    return nc


def _rope_tables():
    t = np.arange(S)
    row = (t // 64).astype(np.float32)
    col = (t % 64).astype(np.float32)
    inv = (np.float32(10000.0) ** (-np.arange(0, 32, 2, dtype=np.float32) / np.float32(32))).astype(np.float32)
    ar = (row[:, None] * inv[None, :]).astype(np.float32)
    ac = (col[:, None] * inv[None, :]).astype(np.float32)
    cr, sr, cc, sc = np.cos(ar), np.sin(ar), np.cos(ac), np.sin(ac)
    c = np.concatenate([cr, cr, cc, cc], axis=1).astype(np.float32)
    s = np.concatenate([-sr, sr, -sc, sc], axis=1).astype(np.float32)
    c = c.reshape(NB, 128, 64).transpose(1, 0, 2).reshape(128, NB * 64)
    s = s.reshape(NB, 128, 64).transpose(1, 0, 2).reshape(128, NB * 64)
    return np.ascontiguousarray(c), np.ascontiguousarray(s)


def _wout_perm():
    idx = np.zeros(1024, dtype=np.int64)
    for c in range(8):
        for p in range(128):
            if c < 4:
                idx[c * 128 + p] = c * 128 + p
            else:
                j = c - 4
                idx[c * 128 + p] = 512 + j * 64 + p if p < 64 else 512 + (4 + j) * 64 + (p - 64)
    return idx


_NC_CACHE = {}


def _get_nc(n_layers):
    if n_layers not in _NC_CACHE:
        _NC_CACHE[n_layers] = build_nc(n_layers)
    return _NC_CACHE[n_layers]


def kernel(x, w_in, w_out, pre_g, post_g, ln_a_g, ln_a_b, spatial_w, spatial_b, q_norm_g, k_norm_g):
    f = lambda a: np.ascontiguousarray(np.asarray(a, dtype=np.float32))
    x = f(x)
    w_in = f(w_in)
    w_out_p = f(np.asarray(w_out)[:, _wout_perm(), :])
    pre_g_t = f(np.asarray(pre_g).reshape(DEPTH, 8, 128).transpose(0, 2, 1))
    post_g = f(post_g)
    ln_g = f(ln_a_g)
    ln_b = f(ln_a_b)
    ws_t = f(np.asarray(spatial_w).transpose(0, 3, 1, 2).reshape(DEPTH, 128, 1024))
    sb_t = f(np.asarray(spatial_b).transpose(0, 2, 1))
    qg = f(q_norm_g)
    kg = f(k_norm_g)
    rc, rs = _rope_tables()
    ident = np.eye(128, dtype=np.float32)
    n_cores = 8
    nl = N_LAYERS_PER_LAUNCH
    nc = _get_nc(nl)
    cur = [np.ascontiguousarray(x[NSEQ * c:NSEQ * (c + 1)]) for c in range(n_cores)]
    for l0 in range(0, DEPTH, nl):
        sl = slice(l0, l0 + nl)
        in_maps = []
        for c in range(n_cores):
            in_maps.append({
                "x": cur[c], "w_in": w_in[sl], "w_out_p": w_out_p[sl], "pre_g_t": pre_g_t[sl], "post_g": post_g[sl],
                "ln_g": ln_g[sl], "ln_b": ln_b[sl], "ws_t": ws_t[sl], "sb_t": sb_t[sl], "qg": qg[sl], "kg": kg[sl],
                "rope_c": rc, "rope_s": rs, "ident": ident,
            })
        res = run_bass_kernel_spmd(nc, in_maps, core_ids=list(range(n_cores)))
        cur = [np.ascontiguousarray(res.results[c]["out"]) for c in range(n_cores)]
    return np.concatenate(cur, axis=0).astype(np.float32)
```

```python
import contextlib
import numpy as np
import concourse.bass as bass
import concourse.mybir as mybir
from concourse.bass_utils import run_bass_kernel_spmd

F32 = mybir.dt.float32
BF16 = mybir.dt.bfloat16
AF = mybir.ActivationFunctionType
ALU = mybir.AluOpType
AX = mybir.AxisListType

D = 1024
S = 4096
NB = S // 128
DEPTH = 4
DIN = 2816
NSEQ = 2
EPS = 1e-6
OU, OVA, OGA, OQ, OK_, OV, OGB = 0, 512, 1024, 1536, 2048, 2176, 2304

N_LAYERS_PER_LAUNCH = 4


class Buf:
    __slots__ = ("name", "w", "r", "psum")

    def __init__(self, name, psum=False):
        self.name = name
        self.w = None
        self.r = {}
        self.psum = psum


class Op:
    __slots__ = ("eng", "fn", "deps", "sem", "val", "ndma", "has_dep", "dma_sem")


class Prog:
    ENGS = ("sp", "act", "pool", "dve", "pe")

    def __init__(self):
        self.q = {e: [] for e in self.ENGS}

    def op(self, eng, fn, reads=(), writes=(), dma_sem=None, ndma=0):
        o = Op()
        o.eng = eng
        o.fn = fn
        o.dma_sem = dma_sem
        o.ndma = ndma
        o.has_dep = False
        o.sem = None
        o.val = 0
        deps = set()
        for b in reads:
            if b.w is not None:
                deps.add(b.w)
            if b.psum:
                for k, r in b.r.items():
                    if k != eng:
                        deps.add(r)
        for b in writes:
            if b.w is not None:
                deps.add(b.w)
            for r in b.r.values():
                deps.add(r)
        if eng == "pe":
            deps = {d for d in deps if d.eng != "pe"}
        o.deps = deps
        for b in writes:
            b.w = o
            b.r = {}
        key = ("dma", id(dma_sem)) if dma_sem is not None else eng
        for b in reads:
            b.r[key] = o
        self.q[eng].append(o)
        return o

    def finalize(self, eng_sems):
        for e in self.ENGS:
            for o in self.q[e]:
                for d in o.deps:
                    d.has_dep = True
        dma_cnt = {}
        for e in self.ENGS:
            cnt = 0
            for o in self.q[e]:
                if o.dma_sem is not None:
                    k = id(o.dma_sem)
                    dma_cnt[k] = dma_cnt.get(k, 0) + 16 * o.ndma
                    o.sem = o.dma_sem
                    o.val = dma_cnt[k]
                elif o.has_dep:
                    cnt += 1
                    o.sem = eng_sems[e]
                    o.val = cnt

    def emit(self, eng, e, eng_sems, final_waits=()):
        waited = {}
        for o in self.q[eng]:
            need = {}
            for d in o.deps:
                k = id(d.sem)
                if waited.get(k, 0) < d.val and need.get(k, (None, 0))[1] < d.val:
                    need[k] = (d.sem, d.val)
            for k, (sem, v) in need.items():
                e.wait_ge(sem, v)
                waited[k] = v
            ins = o.fn(e)
            if o.dma_sem is None and o.has_dep:
                ins.then_inc(eng_sems[eng], 1)
        for sem, v in final_waits:
            e.wait_ge(sem, v)


def build_nc(n_layers):
    nc = bass.Bass("TRN2", target_bir_lowering=False)
    L = n_layers
    x_d = nc.dram_tensor("x", [NSEQ, S, D], F32, kind="ExternalInput").ap()
    out_d = nc.dram_tensor("out", [NSEQ, S, D], F32, kind="ExternalOutput").ap()
    win_d = nc.dram_tensor("w_in", [L, D, DIN], F32, kind="ExternalInput").ap()
    wout_d = nc.dram_tensor("w_out_p", [L, D, D], F32, kind="ExternalInput").ap()
    preg_d = nc.dram_tensor("pre_g_t", [L, 128, 8], F32, kind="ExternalInput").ap()
    postg_d = nc.dram_tensor("post_g", [L, D], F32, kind="ExternalInput").ap()
    lng_d = nc.dram_tensor("ln_g", [L, 512], F32, kind="ExternalInput").ap()
    lnb_d = nc.dram_tensor("ln_b", [L, 512], F32, kind="ExternalInput").ap()
    wst_d = nc.dram_tensor("ws_t", [L, 128, 1024], F32, kind="ExternalInput").ap()
    sbt_d = nc.dram_tensor("sb_t", [L, 128, 8], F32, kind="ExternalInput").ap()
    qg_d = nc.dram_tensor("qg", [L, 64], F32, kind="ExternalInput").ap()
    kg_d = nc.dram_tensor("kg", [L, 64], F32, kind="ExternalInput").ap()
    ropc_d = nc.dram_tensor("rope_c", [128, NB * 64], F32, kind="ExternalInput").ap()
    rops_d = nc.dram_tensor("rope_s", [128, NB * 64], F32, kind="ExternalInput").ap()
    ident_d = nc.dram_tensor("ident", [128, 128], F32, kind="ExternalInput").ap()
    htd = nc.dram_tensor("htd", [NB, 128, 1024], BF16).ap()

    P = Prog()
    es = contextlib.ExitStack()
    with es:
        def sb(name, shape, dt=F32):
            return es.enter_context(nc.sbuf_tensor(name, shape, dt))

        def sem(name):
            return es.enter_context(nc.semaphore(name))

        WIN = sb("WIN", [128, 8, DIN], BF16)
        WOUT = sb("WOUT", [128, 8, D], BF16)
        STG = [sb(f"STG{i}", [128, 2560], F32) for i in range(2)]
        KT = sb("KT", [128, S], BF16)
        VX = sb("VX", [128, NB, 192], BF16)
        ROPC = sb("ROPC", [128, NB, 64], F32)
        ROPS = sb("ROPS", [128, NB, 64], F32)
        IDB = sb("IDB", [128, 128], BF16)
        PREG = sb("PREG", [128, 8], F32)
        POSTG = sb("POSTG", [128, D], F32)
        LNG = sb("LNG", [128, 512], F32)
        LNB = sb("LNB", [128, 512], F32)
        QG = sb("QG", [128, 64], F32)
        KG = sb("KG", [128, 64], F32)
        SBT = sb("SBT", [128, 8], F32)
        WST = sb("WST", [128, 8, 128], BF16)
        NEGH = sb("NEGH", [128, 16], F32)
        RALL = sb("RALL", [128, NB], F32)
        XT = [sb(f"XT{i}", [128, D], F32) for i in range(3)]
        XB = [sb(f"XB{i}", [128, D], BF16) for i in range(2)]
        HT = [sb(f"HT{i}", [128, 8, 128], BF16) for i in range(2)]
        ST = [sb(f"ST{i}", [128, 64], F32) for i in range(2)]
        VA = sb("VA", [128, 512], F32)
        GA = sb("GA", [128, 512], F32)
        UU = sb("UU", [128, 512], F32)
        QS = sb("QS", [128, 512], F32)
        GB = sb("GB", [128, 512], F32)
        EG = sb("EG", [128, 512], F32)
        ZB = sb("ZB", [128, 512], F32)
        T1 = sb("T1", [128, 512], F32)
        T2 = sb("T2", [128, 512], F32)
        VN = sb("VN", [128, 512], BF16)
        YA = sb("YA", [128, 512], BF16)
        QP = sb("QP", [128, 512], BF16)
        GBS = sb("GBS", [128, 512], BF16)
        QT = [sb(f"QT{i}", [128, 512], BF16) for i in range(2)]
        GT = [sb(f"GT{i}", [128, 512], BF16) for i in range(2)]
        NE = 3
        ET = [sb(f"ET{i}", [128, 1024], BF16) for i in range(NE)]
        RD = sb("RD", [128, 512], F32)
        A0 = sb("A0", [128, 512], F32)
        A1 = sb("A1", [128, 512], F32)
        TT = sb("TT", [128, 512], F32)
        YT = [sb(f"YT{i}", [128, 8, 128], BF16) for i in range(2)]
        TMP = sb("TMP", [128, D], F32)
        XO = [sb("XO0", [128, D], F32)]
        JUNK = sb("JUNK", [128, D], BF16)
        KS2 = sb("KS2", [128, 256], F32)
        KT1w = sb("KT1w", [128, 256], F32)
        KT2w = sb("KT2w", [128, 256], F32)
        KRw = sb("KRw", [128, 256], BF16)

        SPS = es.enter_context(nc.psum_tensor("SPS", [128, 2048], F32))
        PO = [es.enter_context(nc.psum_tensor(f"PO{i}", [128, 512], F32)) for i in range(2)]
        PY = es.enter_context(nc.psum_tensor("PY", [128, 512], F32))
        PTR = es.enter_context(nc.psum_tensor("PTR", [128, 1024], BF16))
        PTRF = PTR[:].bitcast(F32)

        eng_sems = {e: sem("s_" + e) for e in Prog.ENGS}
        sm_const = sem("d_const")
        sm_lw = sem("d_lw")
        sm_stg = [sem(f"d_stg{i}") for i in range(2)]
        sm_x = [sem(f"d_x{i}") for i in range(3)]
        sm_xo = [sem(f"d_xo{i}") for i in range(2)]
        sm_hts = [sem(f"d_hts{i}") for i in range(2)]
        sm_htl = [sem(f"d_htl{i}") for i in range(2)]

        def B(name):
            return Buf(name)
        bWIN = [B(f"WIN{k}") for k in range(8)]
        bWOUT = [B(f"WOUT{k}") for k in range(8)]
        bSTG = [B("STG0"), B("STG1")]
        bKT = [B(f"KT{b}") for b in range(NB)]
        bVX = [B(f"VX{b}") for b in range(NB)]
        bCONST = B("CONST")
        bLW = B("LW")
        bWST = B("WST")
        bXT = [B("XT0"), B("XT1"), B("XT2")]
        bXB = [B("XB0"), B("XB1")]
        bHT = [B("HT0"), B("HT1")]
        bST = [[B(f"ST{i}_{j}") for j in range(16)] for i in range(2)]
        names = ["VA", "GA", "UU", "QS", "GB", "EG", "ZB", "T1", "T2", "VN", "YA", "QP", "GBS", "RD", "TT", "TMP", "A0", "A1"]
        bb = {n: B(n) for n in names}
        bKS2, bKT1w, bKT2w, bKRw = B("KS2"), B("KT1w"), B("KT2w"), B("KRw")
        bQT = [B("QT0"), B("QT1")]
        bGT = [B("GT0"), B("GT1")]
        bET = [B(f"ET{i}") for i in range(NE)]
        bYT = [[B(f"YT{i}A"), B(f"YT{i}B")] for i in range(2)]
        bXO = [B("XO0"), B("XO1")]
        bS = [Buf("S0", True), Buf("S1", True)]
        bPO = [Buf("PO0", True), Buf("PO1", True)]
        bY = Buf("PY", True)
        bPTR = Buf("PTR", True)
        bXD = [[B(f"XD{s}_{b}") for b in range(NB)] for s in range(NSEQ)]

        C_SS, C_LN, C_RSTD, C_NRSTD, C_BN, C_MV, C_LNV, C_RSTDV = 0, 1, 2, 3, 4, 10, 12, 13
        C_SSQ, C_LNQ, C_RQ, C_SS2, C_LN2, C_RSTD2 = 16, 24, 32, 40, 42, 43
        def act(fn, reads, writes):
            return P.op("act", fn, reads, writes)

        def dve(fn, reads, writes):
            return P.op("dve", fn, reads, writes)

        def pool(fn, reads, writes):
            return P.op("pool", fn, reads, writes)

        def pe(fn, reads, writes):
            return P.op("pe", fn, reads, writes)

        def dma(eng, sem_h, pairs, reads, writes, **kw):
            def fn(e, pairs=pairs):
                ins = None
                for (o, i) in pairs:
                    ins = e.dma_start(out=o, in_=i, **kw)
                    ins.then_inc(sem_h, 16)
                return ins
            return P.op(eng, fn, reads, writes, dma_sem=sem_h, ndma=len(pairs))

        def rsqrt_chain(slot, c_in, c_ln, c_out, n, scale, bufs_in, buf_out):
            st = ST[slot]
            tmpb = Buf("lnchain")
            pool(lambda e: e.tensor_scalar(out=st[:, c_ln:c_ln + n], in0=st[:, c_in:c_in + n], scalar1=scale,
                                           scalar2=EPS, op0=ALU.mult, op1=ALU.add), bufs_in, [tmpb])
            pool(lambda e: e.tensor_tensor(out=st[:, c_out:c_out + n], in0=st[:, c_ln:c_ln + n], in1=NEGH[:, 0:n],
                                           op=ALU.pow), [tmpb, bCONST], [buf_out])

        dma("sp", sm_const, [(ROPC[:].rearrange("p b d -> p (b d)"), ropc_d),
                             (ROPS[:].rearrange("p b d -> p (b d)"), rops_d),
                             (STG[0][:, 0:128], ident_d)], [], [bCONST, bSTG[0]])
        pool(lambda e: e.memset(NEGH[:, 0:8], -0.5), [], [bCONST])
        pool(lambda e: e.memset(NEGH[:, 8:16], -1.0), [], [bCONST])
        pool(lambda e: e.memset(VX[:].rearrange("p b c -> p (b c)"), 1.0), [], bVX)
        dve(lambda e: e.tensor_copy(out=IDB[:], in_=STG[0][:, 0:128]), [bCONST, bSTG[0]], [bCONST])

        xo_count = [0, 0]
        blk_counter = [0]

        bWKV = B("WKV")
        bHTD = [B(f"HTD{b}") for b in range(NB)]
        bRALL = [B(f"RALL{b}") for b in range(NB)]
        bJUNK = B("JUNK")

        def load_layer_small(l):
            dma("sp", sm_lw, [(PREG[:], preg_d[l]),
                              (POSTG[:], postg_d[l:l + 1, :].partition_broadcast(128)),
                              (LNG[:], lng_d[l:l + 1, :].partition_broadcast(128)),
                              (LNB[:], lnb_d[l:l + 1, :].partition_broadcast(128)),
                              (QG[:], qg_d[l:l + 1, :].partition_broadcast(128)),
                              (KG[:], kg_d[l:l + 1, :].partition_broadcast(128)),
                              (SBT[:], sbt_d[l])], [], [bLW])
            dma("sp", sm_stg[1], [(STG[1][:, kc * 256:(kc + 1) * 256], win_d[l, kc * 128:(kc + 1) * 128, OK_:OK_ + 256])
                                  for kc in range(8)], [], [bSTG[1]])
            dve(lambda e: e.tensor_tensor(out=WIN[:, :, OK_:OK_ + 256],
                                          in0=STG[1][:, 0:2048].rearrange("p (k c) -> p k c", k=8),
                                          in1=PREG[:].unsqueeze(2).to_broadcast([128, 8, 256]), op=ALU.mult),
                [bSTG[1], bLW], [bWKV])
            dma("sp", sm_stg[0], [(STG[0][:, 0:1024], wst_d[l])], [], [bSTG[0]])
            dve(lambda e: e.tensor_copy(out=WST[:].rearrange("p h i -> p (h i)"), in_=STG[0][:, 0:1024]),
                [bSTG[0]], [bWST])

        def load_layer_big(l):
            si = 0
            for kc in range(8):
                dma("sp", sm_stg[si], [(STG[si][:, 0:2048], win_d[l, kc * 128:(kc + 1) * 128, 0:2048]),
                                       (STG[si][:, 2048:2560], win_d[l, kc * 128:(kc + 1) * 128, OGB:OGB + 512])],
                    [], [bSTG[si]])
                if kc % 2 == 0:
                    act(lambda e, kc=kc, si=si: e.activation(out=WIN[:, kc, 0:2048], in_=STG[si][:, 0:2048], func=AF.Copy,
                                                             scale=PREG[:, kc:kc + 1]),
                        [bSTG[si], bLW], [bWIN[kc]])
                    act(lambda e, kc=kc, si=si: e.activation(out=WIN[:, kc, OGB:OGB + 512], in_=STG[si][:, 2048:2560],
                                                             func=AF.Copy, scale=PREG[:, kc:kc + 1]),
                        [bSTG[si], bLW], [bWIN[kc]])
                else:
                    dve(lambda e, kc=kc, si=si: e.tensor_scalar(out=WIN[:, kc, 0:2048], in0=STG[si][:, 0:2048],
                                                                scalar1=PREG[:, kc:kc + 1], scalar2=None,
                                                                op0=ALU.mult),
                        [bSTG[si], bLW], [bWIN[kc]])
                    dve(lambda e, kc=kc, si=si: e.tensor_scalar(out=WIN[:, kc, OGB:OGB + 512], in0=STG[si][:, 2048:2560],
                                                                scalar1=PREG[:, kc:kc + 1], scalar2=None,
                                                                op0=ALU.mult),
                        [bSTG[si], bLW], [bWIN[kc]])
                si ^= 1
                yield
            for kc in range(8):
                dma("sp", sm_stg[si], [(STG[si][:, 0:D], wout_d[l, kc * 128:(kc + 1) * 128, :])], [], [bSTG[si]])
                if kc % 2 == 0:
                    act(lambda e, kc=kc, si=si: e.activation(out=WOUT[:, kc, :], in_=STG[si][:, 0:D], func=AF.Copy),
                        [bSTG[si]], [bWOUT[kc]])
                else:
                    dve(lambda e, kc=kc, si=si: e.tensor_copy(out=WOUT[:, kc, :], in_=STG[si][:, 0:D]),
                        [bSTG[si]], [bWOUT[kc]])
                si ^= 1
                yield

        def block_front(src_d, s, b, slot):
            xt, xb, ht, st = XT[slot], XB[slot], HT[slot], ST[slot]
            dma("sp", sm_x[slot], [(xt[:], src_d[s, b * 128:(b + 1) * 128, :])], [bXD[s][b]], [bXT[slot]])
            act(lambda e: e.activation(out=xb[:], in_=xt[:], func=AF.Copy), [bXT[slot]], [bXB[slot]])
            act(lambda e: e.activation(out=JUNK[:], in_=xt[:], func=AF.Square, accum_out=st[:, C_SS:C_SS + 1]),
                [bXT[slot]], [bST[slot][0], bJUNK])
            rsqrt_chain(slot, C_SS, C_LN, C_RSTD, 1, 1.0 / D, [bST[slot][0]], bST[slot][1])

            def tr(e):
                ins = None
                for kc in range(8):
                    ins = e.transpose(out=PTR[:, kc * 128:(kc + 1) * 128], in_=xb[:, kc * 128:(kc + 1) * 128],
                                      identity=IDB[:])
                return ins
            pe(tr, [bXB[slot], bCONST], [bPTR])
            act(lambda e: e.activation(out=ht[:].rearrange("p k t -> p (k t)"), in_=PTR[:], func=AF.Copy), [bPTR], [bHT[slot]])

        def proj_part(co, n, slot, k0, k1, alt=False):
            ht = HT[slot]
            bank, bbank = (PTRF, bPTR) if alt else (PY, bY)

            def fn(e):
                ins = None
                for kc in range(k0, k1):
                    ins = e.matmul(bank[:, 0:n], lhsT=ht[:, kc, :], rhs=WIN[:, kc, co:co + n],
                                   start=(kc == 0), stop=(kc == 7))
                return ins
            pe(fn, [bHT[slot]] + ([bWKV] if co == OK_ else bWIN[k0:k1]), [bbank])

        def pview(t, g):
            return t[:, 0:512].rearrange("p (j g d) -> p j g d", j=4, g=2, d=64)[:, :, g, :]

        def nview(t, g):
            return t[:, g * 256:(g + 1) * 256].rearrange("p (j d) -> p j d", j=4)

        def rope(src, t1, t2, dst_view, H, b, bsrc, bt1, bt2, bdst, vw=None):
            cb = ROPC[:, b, :].unsqueeze(1).to_broadcast([128, H, 64])
            srs = ROPS[:, b, :].rearrange("p (a t i) -> p a t i", a=2, t=2, i=16)
            s3 = src[:, 0:H * 64].rearrange("p (h d) -> p h d", h=H)
            s5 = src[:, 0:H * 64].rearrange("p (h a t i) -> p h a t i", h=H, a=2, t=2, i=16)
            t15 = t1[:, 0:H * 64].rearrange("p (h d) -> p h d", h=H)
            t25 = t2[:, 0:H * 64].rearrange("p (h a t i) -> p h a t i", h=H, a=2, t=2, i=16)
            dve(lambda e: e.tensor_tensor(out=t15, in0=s3, in1=cb, op=ALU.mult), [bsrc, bCONST], [bt1])
            for t in range(2):
                sv = srs[:, :, t, :].unsqueeze(1).to_broadcast([128, H, 2, 16])
                dve(lambda e, t=t, sv=sv: e.tensor_tensor(out=t25[:, :, :, t, :], in0=s5[:, :, :, 1 - t, :], in1=sv,
                                                          op=ALU.mult), [bsrc, bCONST], [bt2])
            if vw is None:
                dve(lambda e: e.tensor_tensor(out=dst_view, in0=t1[:, 0:H * 64], in1=t2[:, 0:H * 64], op=ALU.add),
                    [bt1, bt2], [bdst])
            else:
                for g in range(2):
                    dve(lambda e, g=g: e.tensor_tensor(out=pview(dst_view, g), in0=nview(t1, g), in1=nview(t2, g),
                                                       op=ALU.add), [bt1, bt2], [bdst])

        P_SS, P_LN, P_RSTD = 44, 46, 48

        def phase1_seq(src_d, s, wgen=None):
            npair = NB // 2
            xsl = lambda i, t: (2 * i + t) % 3

            def loads(i):
                for t in range(2):
                    b = 2 * i + t
                    x3 = xsl(i, t)
                    dma("sp", sm_x[x3], [(XT[x3][:], src_d[s, b * 128:(b + 1) * 128, :])], [bXD[s][b]], [bXT[x3]])
            def do_pair(i):
                b0 = 2 * i
                sl = i % 2
                st = ST[sl]
                bs = bST[sl]
                for t in range(2):
                    x3 = xsl(i, t)
                    dve(lambda e, t=t, x3=x3: e.tensor_copy(out=XB[t][:], in_=XT[x3][:]), [bXT[x3]], [bXB[t]])
                    act(lambda e, t=t, x3=x3: e.activation(out=JUNK[:], in_=XT[x3][:], func=AF.Square,
                                                           accum_out=st[:, P_SS + t:P_SS + t + 1]),
                        [bXT[x3]], [bs[0], bJUNK])
                if i + 1 < npair:
                    loads(i + 1)
                rsqrt_chain(sl, P_SS, P_LN, P_RSTD, 2, 1.0 / D, [bs[0]], bs[1])
                pool(lambda e: e.tensor_copy(out=RALL[:, b0:b0 + 2], in_=st[:, P_RSTD:P_RSTD + 2]), [bs[1]],
                     [bRALL[b0], bRALL[b0 + 1]])
                for t in range(2):
                    def tr(e, t=t):
                        ins = None
                        for kc in range(8):
                            ins = e.transpose(out=PTR[:, kc * 128:(kc + 1) * 128], in_=XB[t][:, kc * 128:(kc + 1) * 128],
                                              identity=IDB[:])
                        return ins
                    pe(tr, [bXB[t], bCONST], [bPTR])
                    act(lambda e, t=t: e.activation(out=HT[t][:].rearrange("p k t -> p (k t)"), in_=PTR[:], func=AF.Copy),
                        [bPTR], [bHT[t]])
                    dma("sp", sm_hts[t], [(htd[b0 + t], HT[t][:].rearrange("p k t -> p (k t)"))], [bHT[t]], [bHTD[b0 + t]])
                for t in range(2):
                    def pj(e, t=t):
                        ins = None
                        for kc in range(8):
                            ins = e.matmul(PY[:, t * 256:(t + 1) * 256], lhsT=HT[t][:, kc, :], rhs=WIN[:, kc, OK_:OK_ + 256],
                                           start=(kc == 0), stop=(kc == 7))
                        return ins
                    pe(pj, [bHT[t], bWKV], [bY])
                rs2 = st[:, P_RSTD:P_RSTD + 2]
                pyv = PY[:].rearrange("p (t c) -> p t c", t=2)
                dve(lambda e: e.tensor_tensor(out=KS2[:].rearrange("p (t c) -> p t c", t=2), in0=pyv[:, :, 0:128],
                                              in1=rs2.unsqueeze(2).to_broadcast([128, 2, 128]), op=ALU.mult),
                    [bY, bs[1]], [bKS2])
                for t in range(2):
                    rst = st[:, P_RSTD + t:P_RSTD + t + 1]
                    act(lambda e, t=t, rst=rst: e.activation(out=VX[:, b0 + t, 0:64], in_=PY[:, t * 256 + 128:t * 256 + 192],
                                                             func=AF.Copy, scale=rst), [bY, bs[1]], [bVX[b0 + t]])
                    act(lambda e, t=t, rst=rst: e.activation(out=VX[:, b0 + t, 128:192], in_=PY[:, t * 256 + 192:t * 256 + 256],
                                                             func=AF.Copy, scale=rst), [bY, bs[1]], [bVX[b0 + t]])
                dve(lambda e: e.tensor_tensor(out=KT1w[:], in0=KS2[:], in1=KS2[:], op=ALU.mult), [bKS2], [bKT1w])
                dve(lambda e: e.tensor_reduce(out=st[:, C_SSQ:C_SSQ + 4], in_=KT1w[:].rearrange("p (h d) -> p h d", h=4),
                                              axis=AX.X, op=ALU.add), [bKT1w], [bs[4]])
                rsqrt_chain(sl, C_SSQ, C_LNQ, C_RQ, 4, 1.0 / 64, [bs[4]], bs[5])
                k4 = KS2[:].rearrange("p (h d) -> p h d", h=4)
                dve(lambda e: e.tensor_tensor(out=k4, in0=k4, in1=st[:, C_RQ:C_RQ + 4].unsqueeze(2).to_broadcast([128, 4, 64]),
                                              op=ALU.mult), [bKS2, bs[5]], [bKS2])
                dve(lambda e: e.tensor_tensor(out=k4, in0=k4, in1=KG[:].unsqueeze(1).to_broadcast([128, 4, 64]),
                                              op=ALU.mult), [bKS2, bLW], [bKS2])
                for t in range(2):
                    dve(lambda e, t=t: e.tensor_tensor(out=KT1w[:, t * 128:(t + 1) * 128].rearrange("p (h d) -> p h d", h=2),
                                                       in0=KS2[:, t * 128:(t + 1) * 128].rearrange("p (h d) -> p h d", h=2),
                                                       in1=ROPC[:, b0 + t, :].unsqueeze(1).to_broadcast([128, 2, 64]),
                                                       op=ALU.mult), [bKS2, bCONST], [bKT1w])
                for t in range(2):
                    srs = ROPS[:, b0 + t, :].rearrange("p (a u i) -> p a u i", a=2, u=2, i=16)
                    s5 = KS2[:, t * 128:(t + 1) * 128].rearrange("p (h a u i) -> p h a u i", h=2, a=2, u=2, i=16)
                    t25 = KT2w[:, t * 128:(t + 1) * 128].rearrange("p (h a u i) -> p h a u i", h=2, a=2, u=2, i=16)
                    for u in range(2):
                        sv = srs[:, :, u, :].unsqueeze(1).to_broadcast([128, 2, 2, 16])
                        dve(lambda e, u=u, sv=sv, s5=s5, t25=t25: e.tensor_tensor(out=t25[:, :, :, u, :],
                                                                                  in0=s5[:, :, :, 1 - u, :], in1=sv,
                                                                                  op=ALU.mult),
                            [bKS2, bCONST], [bKT2w])
                dve(lambda e: e.tensor_tensor(out=KRw[:], in0=KT1w[:], in1=KT2w[:], op=ALU.add), [bKT1w, bKT2w], [bKRw])

                def trk(e):
                    ins = None
                    for t in range(2):
                        ins = e.transpose(out=PTR[:, t * 128:(t + 1) * 128], in_=KRw[:, t * 128:(t + 1) * 128],
                                          identity=IDB[:])
                    return ins
                pe(trk, [bKRw, bCONST], [bPTR])
                dve(lambda e: e.tensor_copy(out=KT[:, b0 * 128:(b0 + 2) * 128], in_=PTR[:, 0:256]), [bPTR],
                    [bKT[b0], bKT[b0 + 1]])
                if wgen is not None:
                    next(wgen, None)
            loads(0)
            for i in range(npair):
                do_pair(i)
            if wgen is not None:
                for _ in wgen:
                    pass

        def tr4(src):
            def fn(e):
                ins = None
                for c in range(4):
                    ins = e.transpose(out=PTR[:, c * 128:(c + 1) * 128], in_=src[:, c * 128:(c + 1) * 128],
                                      identity=IDB[:])
                return ins
            return fn

        def tanh_half(src, bsrc):
            act(lambda e: e.activation(out=EG[:], in_=src[:], func=AF.Tanh, scale=0.5), [bsrc], [bb["EG"]])

        def planA(src_d, s, b, slot, x3, with_f0=True, only_f0=False):
            st = ST[slot]
            bs = bST[slot]
            yt = YT[slot]
            rstd = RALL[:, b:b + 1]
            plan = {}

            def at(c, fn):
                plan.setdefault(c, []).append(fn)

            def f0a():
                dma("sp", sm_x[x3], [(XT[x3][:], src_d[s, b * 128:(b + 1) * 128, :])], [bXD[s][b]], [bXT[x3]])
                dma("sp", sm_htl[slot], [(HT[slot][:].rearrange("p k t -> p (k t)"), htd[b])], [bHTD[b]], [bHT[slot]])

            if only_f0:
                return {27: [f0a]}
            if with_f0:
                at(0, f0a)

            def group(c, co, dst, nm, alt=False):
                bank, bbank = (PTRF, bPTR) if alt else (PY, bY)
                for i in range(4):
                    if i < 3:
                        at(c + i, lambda i=i: proj_part(co, 512, slot, 2 * i, 2 * i + 2, alt))
                    else:
                        def last():
                            proj_part(co, 512, slot, 6, 8, alt)
                            dve(lambda e: e.tensor_scalar(out=dst[:], in0=bank[:, 0:512], scalar1=rstd, scalar2=None,
                                                          op0=ALU.mult), [bbank, bRALL[b]], [bb[nm]])
                        at(c + 3, last)
            group(3, OVA, VA, "VA")
            group(12, OQ, QS, "QS")
            group(21, OGA, GA, "GA", alt=True)
            group(24, OU, UU, "UU")
            group(28, OGB, GB, "GB", alt=True)

            def ln():
                dve(lambda e: e.bn_stats(out=st[:, C_BN:C_BN + 6], in_=VA[:]), [bb["VA"]], [bs[6]])
                dve(lambda e: e.bn_aggr(out=st[:, C_MV:C_MV + 2], in_=st[:, C_BN:C_BN + 6]), [bs[6]], [bs[7]])
                rsqrt_chain(slot, C_MV + 1, C_LNV, C_RSTDV, 1, 1.0, [bs[7]], bs[8])
                dve(lambda e: e.scalar_tensor_tensor(out=VA[:], in0=VA[:], scalar=st[:, C_MV:C_MV + 1], in1=LNG[:],
                                                     op0=ALU.subtract, op1=ALU.mult), [bb["VA"], bs[7], bLW], [bb["VA"]])
            at(7, ln)
            at(9, lambda: dve(lambda e: e.scalar_tensor_tensor(out=VN[:], in0=VA[:], scalar=st[:, C_RSTDV:C_RSTDV + 1],
                                                                in1=LNB[:], op0=ALU.mult, op1=ALU.add),
                               [bb["VA"], bs[8], bLW], [bb["VN"]]))

            def q1():
                dve(lambda e: e.tensor_tensor(out=T1[:], in0=QS[:], in1=QS[:], op=ALU.mult), [bb["QS"]], [bb["T1"]])
                dve(lambda e: e.tensor_reduce(out=st[:, C_SSQ:C_SSQ + 8], in_=T1[:].rearrange("p (h d) -> p h d", h=8),
                                              axis=AX.X, op=ALU.add), [bb["T1"]], [bs[4]])
                rsqrt_chain(slot, C_SSQ, C_LNQ, C_RQ, 8, 1.0 / 64, [bs[4]], bs[5])
            at(16, q1)

            def q2():
                dve(lambda e: e.tensor_tensor(out=QS[:].rearrange("p (h d) -> p h d", h=8),
                                              in0=QS[:].rearrange("p (h d) -> p h d", h=8),
                                              in1=st[:, C_RQ:C_RQ + 8].unsqueeze(2).to_broadcast([128, 8, 64]),
                                              op=ALU.mult), [bb["QS"], bs[5]], [bb["QS"]])
                dve(lambda e: e.tensor_tensor(out=QS[:].rearrange("p (h d) -> p h d", h=8),
                                              in0=QS[:].rearrange("p (h d) -> p h d", h=8),
                                              in1=QG[:].unsqueeze(1).to_broadcast([128, 8, 64]),
                                              op=ALU.mult), [bb["QS"], bLW], [bb["QS"]])
            at(18, q2)
            at(21, lambda: rope(QS, T1, T2, QP, 8, b, bb["QS"], bb["T1"], bb["T2"], bb["QP"], vw=True))

            def zz():
                def zmm(e):
                    ins = None
                    for h in range(8):
                        ins = e.matmul(PY[:, h * 64:(h + 1) * 64], lhsT=WST[:, h, :], rhs=VN[:, h * 64:(h + 1) * 64],
                                       start=True, stop=True)
                    return ins
                pe(zmm, [bb["VN"], bWST], [bY])
                dve(lambda e: e.tensor_tensor(out=ZB[:].rearrange("p (h d) -> p h d", h=8),
                                              in0=PY[:].rearrange("p (h d) -> p h d", h=8),
                                              in1=SBT[:].unsqueeze(2).to_broadcast([128, 8, 64]), op=ALU.add),
                    [bY, bLW], [bb["ZB"]])
            at(20, zz)

            def trq():
                pe(tr4(QP), [bb["QP"], bCONST], [bPTR])
                dve(lambda e: e.tensor_copy(out=QT[slot][:], in_=PTR[:, 0:512]), [bPTR], [bQT[slot]])
            at(26, trq)
            at(26, lambda: tanh_half(GA, bb["GA"]))

            def gate_a():
                dve(lambda e: e.scalar_tensor_tensor(out=GA[:], in0=EG[:], scalar=1.0, in1=GA[:], op0=ALU.add,
                                                     op1=ALU.mult), [bb["GA"], bb["EG"]], [bb["GA"]])
                dve(lambda e: e.scalar_tensor_tensor(out=UU[:], in0=UU[:], scalar=0.5, in1=GA[:], op0=ALU.mult,
                                                     op1=ALU.mult), [bb["UU"], bb["GA"]], [bb["UU"]])
                dve(lambda e: e.tensor_tensor(out=YA[:], in0=UU[:], in1=ZB[:], op=ALU.mult), [bb["UU"], bb["ZB"]],
                    [bb["YA"]])
            at(29, gate_a)
            at(33, lambda: tanh_half(GB, bb["GB"]))

            def gate_b():
                for g in range(2):
                    dve(lambda e, g=g: e.scalar_tensor_tensor(out=pview(GBS, g), in0=nview(EG, g), scalar=1.0,
                                                              in1=nview(GB, g), op0=ALU.add, op1=ALU.mult),
                        [bb["GB"], bb["EG"]], [bb["GBS"]])
            at(36, gate_b)

            def tra():
                pe(tr4(YA), [bb["YA"], bCONST], [bPTR])
                dve(lambda e: e.tensor_copy(out=yt[:, 0:4, :].rearrange("p k t -> p (k t)"), in_=PTR[:, 0:512]),
                    [bPTR], [bYT[slot][0]])
            at(35, tra)

            def trg():
                pe(tr4(GBS), [bb["GBS"], bCONST], [bPTR])
                dve(lambda e: e.tensor_copy(out=GT[slot][:], in_=PTR[:, 0:512]), [bPTR], [bGT[slot]])
            at(39, trg)
            return plan

        def planC(s, b, slot, x3):
            st = ST[slot]
            bs = bST[slot]
            yt = YT[slot]
            plan = {}

            def at(c, fn):
                plan.setdefault(c, []).append(fn)

            def op_part(n, k0, k1, evac):
                def fn():
                    def oproj(e):
                        ins = None
                        for kc in range(k0, k1):
                            ins = e.matmul(PTRF[:, 0:512], lhsT=yt[:, kc, :], rhs=WOUT[:, kc, n * 512:(n + 1) * 512],
                                           start=(kc == 0), stop=(kc == 7))
                        return ins
                    pe(oproj, [bYT[slot][0], bYT[slot][1]] + bWOUT[k0:k1], [bPTR])
                    if evac:
                        dve(lambda e: e.tensor_copy(out=TMP[:, n * 512:(n + 1) * 512], in_=PTRF[:, 0:512]), [bPTR],
                            [bb["TMP"]])
                return fn
            for n, c0 in ((0, 10), (1, 16)):
                for i in range(4):
                    at(c0 + i, op_part(n, 2 * i, 2 * i + 2, i == 3))
            xs = 0
            at(21, lambda: act(lambda e: e.activation(out=JUNK[:], in_=TMP[:], func=AF.Square,
                                                      accum_out=st[:, C_SS2:C_SS2 + 1]), [bb["TMP"]], [bs[11], bJUNK]))
            at(23, lambda: rsqrt_chain(slot, C_SS2, C_LN2, C_RSTD2, 1, 1.0 / D, [bs[11]], bs[12]))

            def fin():
                dve(lambda e: e.scalar_tensor_tensor(out=XO[xs][:], in0=TMP[:], scalar=st[:, C_RSTD2:C_RSTD2 + 1],
                                                     in1=POSTG[:], op0=ALU.mult, op1=ALU.mult),
                    [bb["TMP"], bs[12], bLW], [bXO[xs]])
                dve(lambda e: e.tensor_tensor(out=XO[xs][:], in0=XO[xs][:], in1=XT[x3][:], op=ALU.add),
                    [bXO[xs], bXT[x3]], [bXO[xs]])
            at(25, fin)
            at(26, lambda: dma("pool", sm_xo[xs], [(out_d[s, b * 128:(b + 1) * 128, :], XO[xs][:])], [bXO[xs]],
                               [bXD[s][b]]))
            return plan

        def run_plan(plans, c):
            for p in plans:
                for fn in p.get(c, ()):
                    fn()

        def attention(b, slot, plans):
            qt = QT[slot]

            def qk(c):
                par = c % 2

                def fn(e):
                    ins = None
                    for g in range(2):
                        ins = e.matmul(SPS[:, par * 1024 + g * 512: par * 1024 + (g + 1) * 512],
                                       lhsT=KT[g * 64:(g + 1) * 64, c * 128:(c + 1) * 128],
                                       rhs=qt[g * 64:(g + 1) * 64, :], start=True, stop=True,
                                       tile_position=(g * 64, 0))
                    return ins
                pe(fn, [bKT[c], bQT[slot]], [bS[par]])
            qk(0)
            for c in range(NB):
                par = c % 2
                k = c % NE
                act(lambda e, k=k, par=par: e.activation(out=ET[k][:], in_=SPS[:, par * 1024:(par + 1) * 1024],
                                                         func=AF.Exp, scale=0.125), [bS[par]], [bET[k]])
                if c + 1 < NB:
                    qk(c + 1)

                def pv(e, c=c, k=k):
                    ins = None
                    for g in range(2):
                        ins = e.matmul(PO[g][:], lhsT=VX[:, c, g * 64:g * 64 + 128], rhs=ET[k][:, g * 512:(g + 1) * 512],
                                       start=(c == 0), stop=(c == NB - 1))
                    return ins
                run_plan(plans, c)
                pe(pv, [bVX[c], bET[k]], [bPO[0], bPO[1]])

        def stageC0(slot):
            dve(lambda e: e.tensor_copy(out=A0[:], in_=PO[0][:]), [bPO[0]], [bb["A0"]])
            dve(lambda e: e.tensor_copy(out=A1[:], in_=PO[1][:]), [bPO[1]], [bb["A1"]])

        def planC0(slot):
            yt = YT[slot]
            plan = {}

            def at(c, fn):
                plan.setdefault(c, []).append(fn)

            def asm():
                dve(lambda e: e.tensor_copy(out=RD[0:64, :], in_=A0[64:128, :]), [bb["A0"]], [bb["RD"]])
                dve(lambda e: e.tensor_copy(out=RD[64:128, :], in_=A1[0:64, :]), [bb["A1"]], [bb["RD"]])
            at(3, asm)

            def rec(j):
                return lambda: dve(lambda e: e.reciprocal(out=RD[:, j * 128:(j + 1) * 128], in_=RD[:, j * 128:(j + 1) * 128]),
                                   [bb["RD"]], [bb["RD"]])
            at(4, rec(0))
            at(4, rec(1))
            at(5, rec(2))
            at(5, rec(3))

            def fin0():
                dve(lambda e: e.tensor_tensor(out=TT[0:64, :], in0=A0[0:64, :], in1=RD[0:64, :], op=ALU.mult),
                    [bb["A0"], bb["RD"]], [bb["TT"]])
                dve(lambda e: e.tensor_tensor(out=TT[64:128, :], in0=A1[64:128, :], in1=RD[64:128, :], op=ALU.mult),
                    [bb["A1"], bb["RD"]], [bb["TT"]])
                dve(lambda e: e.scalar_tensor_tensor(out=yt[:, 4:8, :].rearrange("p k t -> p (k t)"), in0=TT[:],
                                                     scalar=0.5, in1=GT[slot][:], op0=ALU.mult, op1=ALU.mult),
                    [bb["TT"], bGT[slot]], [bYT[slot][1]])
            at(6, fin0)
            return plan

        def split_plan(p):
            now = {c: f for c, f in p.items() if c < NB}
            later = {c - NB: f for c, f in p.items() if c >= NB}
            return now, later

        def phase2_seq(src_d, s):
            slots = []
            for b in range(NB):
                slots.append(blk_counter[0] % 2)
                blk_counter[0] += 1
            p0 = planA(src_d, s, 0, slots[0], 0)
            for c in sorted(p0):
                run_plan([p0], c)
            prevC = None
            carry = None
            for b in range(NB):
                plans = []
                if prevC is not None:
                    plans.append(prevC0)
                    plans.append(prevC)
                if carry:
                    plans.append(carry)
                carry = None
                if b + 1 < NB:
                    now, carry = split_plan(planA(src_d, s, b + 1, slots[b + 1], (b + 1) % 3, with_f0=(b == 0)))
                    plans.append(now)
                if b + 2 < NB:
                    plans.append(planA(src_d, s, b + 2, slots[b + 2], (b + 2) % 3, only_f0=True))
                attention(b, slots[b], plans)
                stageC0(slots[b])
                prevC0 = planC0(slots[b])
                prevC = planC(s, b, slots[b], b % 3)
            for c in sorted(set(prevC) | set(prevC0)):
                run_plan([prevC0, prevC], c)

        import os
        stage = int(os.environ.get("K_STAGE", "99"))
        for l in range(L):
            src = x_d if l == 0 else out_d
            load_layer_small(l)
            wgen = load_layer_big(l)
            if stage == 0:
                for _ in wgen:
                    pass
                continue
            for s in range(NSEQ):
                phase1_seq(src, s, wgen if s == 0 else None)
                if stage == 1:
                    break
                phase2_seq(src, s)
                if stage == 2:
                    break

        P.finalize(eng_sems)
        last_xo = {}
        for o in P.q["pool"]:
            if o.dma_sem is not None:
                last_xo[id(o.dma_sem)] = (o.dma_sem, o.val)
        finals = list(last_xo.values())

        block = es.enter_context(nc.Block())

        @block.sync
        def _(e):
            P.emit("sp", e, eng_sems)

        @block.scalar
        def _(e):
            P.emit("act", e, eng_sems)

        @block.vector
        def _(e):
            P.emit("dve", e, eng_sems)

        @block.tensor
        def _(e):
            P.emit("pe", e, eng_sems)

        @block.gpsimd
        def _(e):
            P.emit("pool", e, eng_sems, final_waits=finals)
    return nc


def _rope_tables():
    t = np.arange(S)
    row = (t // 64).astype(np.float32)
    col = (t % 64).astype(np.float32)
    inv = (np.float32(10000.0) ** (-np.arange(0, 32, 2, dtype=np.float32) / np.float32(32))).astype(np.float32)
    ar = (row[:, None] * inv[None, :]).astype(np.float32)
    ac = (col[:, None] * inv[None, :]).astype(np.float32)
    cr, sr, cc, sc = np.cos(ar), np.sin(ar), np.cos(ac), np.sin(ac)
    c = np.concatenate([cr, cr, cc, cc], axis=1).astype(np.float32)
    s = np.concatenate([-sr, sr, -sc, sc], axis=1).astype(np.float32)
    c = c.reshape(NB, 128, 64).transpose(1, 0, 2).reshape(128, NB * 64)
    s = s.reshape(NB, 128, 64).transpose(1, 0, 2).reshape(128, NB * 64)
    return np.ascontiguousarray(c), np.ascontiguousarray(s)


def _wout_perm():
    idx = np.zeros(1024, dtype=np.int64)
    for c in range(8):
        for p in range(128):
            if c < 4:
                idx[c * 128 + p] = c * 128 + p
            else:
                j = c - 4
                idx[c * 128 + p] = 512 + j * 64 + p if p < 64 else 512 + (4 + j) * 64 + (p - 64)
    return idx


_NC_CACHE = {}


def _get_nc(n_layers):
    if n_layers not in _NC_CACHE:
        _NC_CACHE[n_layers] = build_nc(n_layers)
    return _NC_CACHE[n_layers]


def kernel(x, w_in, w_out, pre_g, post_g, ln_a_g, ln_a_b, spatial_w, spatial_b, q_norm_g, k_norm_g):
    f = lambda a: np.ascontiguousarray(np.asarray(a, dtype=np.float32))
    x = f(x)
    w_in = f(w_in)
    w_out_p = f(np.asarray(w_out)[:, _wout_perm(), :])
    pre_g_t = f(np.asarray(pre_g).reshape(DEPTH, 8, 128).transpose(0, 2, 1))
    post_g = f(post_g)
    ln_g = f(ln_a_g)
    ln_b = f(ln_a_b)
    ws_t = f(np.asarray(spatial_w).transpose(0, 3, 1, 2).reshape(DEPTH, 128, 1024))
    sb_t = f(np.asarray(spatial_b).transpose(0, 2, 1))
    qg = f(q_norm_g)
    kg = f(k_norm_g)
    rc, rs = _rope_tables()
    ident = np.eye(128, dtype=np.float32)
    n_cores = 8
    nl = N_LAYERS_PER_LAUNCH
    nc = _get_nc(nl)
    cur = [np.ascontiguousarray(x[NSEQ * c:NSEQ * (c + 1)]) for c in range(n_cores)]
    for l0 in range(0, DEPTH, nl):
        sl = slice(l0, l0 + nl)
        in_maps = []
        for c in range(n_cores):
            in_maps.append({
                "x": cur[c], "w_in": w_in[sl], "w_out_p": w_out_p[sl], "pre_g_t": pre_g_t[sl], "post_g": post_g[sl],
                "ln_g": ln_g[sl], "ln_b": ln_b[sl], "ws_t": ws_t[sl], "sb_t": sb_t[sl], "qg": qg[sl], "kg": kg[sl],
                "rope_c": rc, "rope_s": rs, "ident": ident,
            })
        res = run_bass_kernel_spmd(nc, in_maps, core_ids=list(range(n_cores)))
        cur = [np.ascontiguousarray(res.results[c]["out"]) for c in range(n_cores)]
    return np.concatenate(cur, axis=0).astype(np.float32)
```

```python
import contextlib
import numpy as np
import concourse.bass as bass
import concourse.mybir as mybir
from concourse.bass_utils import run_bass_kernel_spmd

F32 = mybir.dt.float32
BF16 = mybir.dt.bfloat16
AF = mybir.ActivationFunctionType
ALU = mybir.AluOpType
AX = mybir.AxisListType

D = 1024
S = 4096
NB = S // 128
DEPTH = 4
DIN = 2816
NSEQ = 2
EPS = 1e-6
OU, OVA, OGA, OQ, OK_, OV, OGB = 0, 512, 1024, 1536, 2048, 2176, 2304

N_LAYERS_PER_LAUNCH = 4


class Buf:
    __slots__ = ("name", "w", "r", "psum")

    def __init__(self, name, psum=False):
        self.name = name
        self.w = None
        self.r = {}
        self.psum = psum


class Op:
    __slots__ = ("eng", "fn", "deps", "sem", "val", "ndma", "has_dep", "dma_sem")


class Prog:
    ENGS = ("sp", "act", "pool", "dve", "pe")

    def __init__(self):
        self.q = {e: [] for e in self.ENGS}

    def op(self, eng, fn, reads=(), writes=(), dma_sem=None, ndma=0):
        o = Op()
        o.eng = eng
        o.fn = fn
        o.dma_sem = dma_sem
        o.ndma = ndma
        o.has_dep = False
        o.sem = None
        o.val = 0
        deps = set()
        for b in reads:
            if b.w is not None:
                deps.add(b.w)
            if b.psum:
                for k, r in b.r.items():
                    if k != eng:
                        deps.add(r)
        for b in writes:
            if b.w is not None:
                deps.add(b.w)
            for r in b.r.values():
                deps.add(r)
        if eng == "pe":
            deps = {d for d in deps if d.eng != "pe"}
        o.deps = deps
        for b in writes:
            b.w = o
            b.r = {}
        key = ("dma", id(dma_sem)) if dma_sem is not None else eng
        for b in reads:
            b.r[key] = o
        self.q[eng].append(o)
        return o

    def finalize(self, eng_sems):
        for e in self.ENGS:
            for o in self.q[e]:
                for d in o.deps:
                    d.has_dep = True
        dma_cnt = {}
        for e in self.ENGS:
            cnt = 0
            for o in self.q[e]:
                if o.dma_sem is not None:
                    k = id(o.dma_sem)
                    dma_cnt[k] = dma_cnt.get(k, 0) + 16 * o.ndma
                    o.sem = o.dma_sem
                    o.val = dma_cnt[k]
                elif o.has_dep:
                    cnt += 1
                    o.sem = eng_sems[e]
                    o.val = cnt

    def emit(self, eng, e, eng_sems, final_waits=()):
        waited = {}
        for o in self.q[eng]:
            need = {}
            for d in o.deps:
                k = id(d.sem)
                if waited.get(k, 0) < d.val and need.get(k, (None, 0))[1] < d.val:
                    need[k] = (d.sem, d.val)
            for k, (sem, v) in need.items():
                e.wait_ge(sem, v)
                waited[k] = v
            ins = o.fn(e)
            if o.dma_sem is None and o.has_dep:
                ins.then_inc(eng_sems[eng], 1)
        for sem, v in final_waits:
            e.wait_ge(sem, v)


def build_nc(n_layers):
    nc = bass.Bass("TRN2", target_bir_lowering=False)
    L = n_layers
    x_d = nc.dram_tensor("x", [NSEQ, S, D], F32, kind="ExternalInput").ap()
    out_d = nc.dram_tensor("out", [NSEQ, S, D], F32, kind="ExternalOutput").ap()
    win_d = nc.dram_tensor("w_in", [L, D, DIN], F32, kind="ExternalInput").ap()
    wout_d = nc.dram_tensor("w_out_p", [L, D, D], F32, kind="ExternalInput").ap()
    preg_d = nc.dram_tensor("pre_g_t", [L, 128, 8], F32, kind="ExternalInput").ap()
    postg_d = nc.dram_tensor("post_g", [L, D], F32, kind="ExternalInput").ap()
    lng_d = nc.dram_tensor("ln_g", [L, 512], F32, kind="ExternalInput").ap()
    lnb_d = nc.dram_tensor("ln_b", [L, 512], F32, kind="ExternalInput").ap()
    wst_d = nc.dram_tensor("ws_t", [L, 128, 1024], F32, kind="ExternalInput").ap()
    sbt_d = nc.dram_tensor("sb_t", [L, 128, 8], F32, kind="ExternalInput").ap()
    qg_d = nc.dram_tensor("qg", [L, 64], F32, kind="ExternalInput").ap()
    kg_d = nc.dram_tensor("kg", [L, 64], F32, kind="ExternalInput").ap()
    ropc_d = nc.dram_tensor("rope_c", [128, NB * 64], F32, kind="ExternalInput").ap()
    rops_d = nc.dram_tensor("rope_s", [128, NB * 64], F32, kind="ExternalInput").ap()
    ident_d = nc.dram_tensor("ident", [128, 128], F32, kind="ExternalInput").ap()
    htd = nc.dram_tensor("htd", [NB, 128, 1024], BF16).ap()

    P = Prog()
    es = contextlib.ExitStack()
    with es:
        def sb(name, shape, dt=F32):
            return es.enter_context(nc.sbuf_tensor(name, shape, dt))

        def sem(name):
            return es.enter_context(nc.semaphore(name))

        WIN = sb("WIN", [128, 8, DIN], BF16)
        WOUT = sb("WOUT", [128, 8, D], BF16)
        STG = [sb(f"STG{i}", [128, 2560], F32) for i in range(2)]
        KT = sb("KT", [128, S], BF16)
        VX = sb("VX", [128, NB, 192], BF16)
        ROPC = sb("ROPC", [128, NB, 64], F32)
        ROPS = sb("ROPS", [128, NB, 64], F32)
        IDB = sb("IDB", [128, 128], BF16)
        PREG = sb("PREG", [128, 8], F32)
        POSTG = sb("POSTG", [128, D], F32)
        LNG = sb("LNG", [128, 512], F32)
        LNB = sb("LNB", [128, 512], F32)
        QG = sb("QG", [128, 64], F32)
        KG = sb("KG", [128, 64], F32)
        SBT = sb("SBT", [128, 8], F32)
        WST = sb("WST", [128, 8, 128], BF16)
        NEGH = sb("NEGH", [128, 16], F32)
        RALL = sb("RALL", [128, NB], F32)
        XT = [sb(f"XT{i}", [128, D], F32) for i in range(3)]
        XB = [sb(f"XB{i}", [128, D], BF16) for i in range(2)]
        HT = [sb(f"HT{i}", [128, 8, 128], BF16) for i in range(2)]
        ST = [sb(f"ST{i}", [128, 64], F32) for i in range(2)]
        VA = sb("VA", [128, 512], F32)
        GA = sb("GA", [128, 512], F32)
        UU = sb("UU", [128, 512], F32)
        QS = sb("QS", [128, 512], F32)
        GB = sb("GB", [128, 512], F32)
        EG = sb("EG", [128, 512], F32)
        ZB = sb("ZB", [128, 512], F32)
        T1 = sb("T1", [128, 512], F32)
        T2 = sb("T2", [128, 512], F32)
        VN = sb("VN", [128, 512], BF16)
        YA = sb("YA", [128, 512], BF16)
        QP = sb("QP", [128, 512], BF16)
        GBS = sb("GBS", [128, 512], BF16)
        QT = [sb(f"QT{i}", [128, 512], BF16) for i in range(2)]
        GT = [sb(f"GT{i}", [128, 512], BF16) for i in range(2)]
        NE = 3
        ET = [sb(f"ET{i}", [128, 1024], BF16) for i in range(NE)]
        RD = sb("RD", [128, 512], F32)
        A0 = sb("A0", [128, 512], F32)
        A1 = sb("A1", [128, 512], F32)
        TT = sb("TT", [128, 512], F32)
        YT = [sb(f"YT{i}", [128, 8, 128], BF16) for i in range(2)]
        TMP = sb("TMP", [128, D], F32)
        XO = [sb("XO0", [128, D], F32)]
        JUNK = sb("JUNK", [128, D], BF16)
        KS2 = sb("KS2", [128, 256], F32)
        KT1w = sb("KT1w", [128, 256], F32)
        KT2w = sb("KT2w", [128, 256], F32)
        KRw = sb("KRw", [128, 256], BF16)

        SPS = es.enter_context(nc.psum_tensor("SPS", [128, 2048], F32))
        PO = [es.enter_context(nc.psum_tensor(f"PO{i}", [128, 512], F32)) for i in range(2)]
        PY = es.enter_context(nc.psum_tensor("PY", [128, 512], F32))
        PTR = es.enter_context(nc.psum_tensor("PTR", [128, 1024], BF16))
        PTRF = PTR[:].bitcast(F32)

        eng_sems = {e: sem("s_" + e) for e in Prog.ENGS}
        sm_const = sem("d_const")
        sm_lw = sem("d_lw")
        sm_stg = [sem(f"d_stg{i}") for i in range(2)]
        sm_x = [sem(f"d_x{i}") for i in range(3)]
        sm_xo = [sem(f"d_xo{i}") for i in range(2)]
        sm_hts = [sem(f"d_hts{i}") for i in range(2)]
        sm_htl = [sem(f"d_htl{i}") for i in range(2)]

        def B(name):
            return Buf(name)
        bWIN = [B(f"WIN{k}") for k in range(8)]
        bWOUT = [B(f"WOUT{k}") for k in range(8)]
        bSTG = [B("STG0"), B("STG1")]
        bKT = [B(f"KT{b}") for b in range(NB)]
        bVX = [B(f"VX{b}") for b in range(NB)]
        bCONST = B("CONST")
        bLW = B("LW")
        bWST = B("WST")
        bXT = [B("XT0"), B("XT1"), B("XT2")]
        bXB = [B("XB0"), B("XB1")]
        bHT = [B("HT0"), B("HT1")]
        bST = [[B(f"ST{i}_{j}") for j in range(16)] for i in range(2)]
        names = ["VA", "GA", "UU", "QS", "GB", "EG", "ZB", "T1", "T2", "VN", "YA", "QP", "GBS", "RD", "TT", "TMP", "A0", "A1"]
        bb = {n: B(n) for n in names}
        bKS2, bKT1w, bKT2w, bKRw = B("KS2"), B("KT1w"), B("KT2w"), B("KRw")
        bQT = [B("QT0"), B("QT1")]
        bGT = [B("GT0"), B("GT1")]
        bET = [B(f"ET{i}") for i in range(NE)]
        bYT = [[B(f"YT{i}A"), B(f"YT{i}B")] for i in range(2)]
        bXO = [B("XO0"), B("XO1")]
        bS = [Buf("S0", True), Buf("S1", True)]
        bPO = [Buf("PO0", True), Buf("PO1", True)]
        bY = Buf("PY", True)
        bPTR = Buf("PTR", True)
        bXD = [[B(f"XD{s}_{b}") for b in range(NB)] for s in range(NSEQ)]

        C_SS, C_LN, C_RSTD, C_NRSTD, C_BN, C_MV, C_LNV, C_RSTDV = 0, 1, 2, 3, 4, 10, 12, 13
        C_SSQ, C_LNQ, C_RQ, C_SS2, C_LN2, C_RSTD2 = 16, 24, 32, 40, 42, 43
        def act(fn, reads, writes):
            return P.op("act", fn, reads, writes)

        def dve(fn, reads, writes):
            return P.op("dve", fn, reads, writes)

        def pool(fn, reads, writes):
            return P.op("pool", fn, reads, writes)

        def pe(fn, reads, writes):
            return P.op("pe", fn, reads, writes)

        def dma(eng, sem_h, pairs, reads, writes, **kw):
            def fn(e, pairs=pairs):
                ins = None
                for (o, i) in pairs:
                    ins = e.dma_start(out=o, in_=i, **kw)
                    ins.then_inc(sem_h, 16)
                return ins
            return P.op(eng, fn, reads, writes, dma_sem=sem_h, ndma=len(pairs))

        def rsqrt_chain(slot, c_in, c_ln, c_out, n, scale, bufs_in, buf_out):
            st = ST[slot]
            tmpb = Buf("lnchain")
            pool(lambda e: e.tensor_scalar(out=st[:, c_ln:c_ln + n], in0=st[:, c_in:c_in + n], scalar1=scale,
                                           scalar2=EPS, op0=ALU.mult, op1=ALU.add), bufs_in, [tmpb])
            pool(lambda e: e.tensor_tensor(out=st[:, c_out:c_out + n], in0=st[:, c_ln:c_ln + n], in1=NEGH[:, 0:n],
                                           op=ALU.pow), [tmpb, bCONST], [buf_out])

        dma("sp", sm_const, [(ROPC[:].rearrange("p b d -> p (b d)"), ropc_d),
                             (ROPS[:].rearrange("p b d -> p (b d)"), rops_d),
                             (STG[0][:, 0:128], ident_d)], [], [bCONST, bSTG[0]])
        pool(lambda e: e.memset(NEGH[:, 0:8], -0.5), [], [bCONST])
        pool(lambda e: e.memset(NEGH[:, 8:16], -1.0), [], [bCONST])
        pool(lambda e: e.memset(VX[:].rearrange("p b c -> p (b c)"), 1.0), [], bVX)
        dve(lambda e: e.tensor_copy(out=IDB[:], in_=STG[0][:, 0:128]), [bCONST, bSTG[0]], [bCONST])

        xo_count = [0, 0]
        blk_counter = [0]

        bWKV = B("WKV")
        bHTD = [B(f"HTD{b}") for b in range(NB)]
        bRALL = [B(f"RALL{b}") for b in range(NB)]
        bJUNK = B("JUNK")

        def load_layer_small(l):
            dma("sp", sm_lw, [(PREG[:], preg_d[l]),
                              (POSTG[:], postg_d[l:l + 1, :].partition_broadcast(128)),
                              (LNG[:], lng_d[l:l + 1, :].partition_broadcast(128)),
                              (LNB[:], lnb_d[l:l + 1, :].partition_broadcast(128)),
                              (QG[:], qg_d[l:l + 1, :].partition_broadcast(128)),
                              (KG[:], kg_d[l:l + 1, :].partition_broadcast(128)),
                              (SBT[:], sbt_d[l])], [], [bLW])
            dma("sp", sm_stg[1], [(STG[1][:, kc * 256:(kc + 1) * 256], win_d[l, kc * 128:(kc + 1) * 128, OK_:OK_ + 256])
                                  for kc in range(8)], [], [bSTG[1]])
            dve(lambda e: e.tensor_tensor(out=WIN[:, :, OK_:OK_ + 256],
                                          in0=STG[1][:, 0:2048].rearrange("p (k c) -> p k c", k=8),
                                          in1=PREG[:].unsqueeze(2).to_broadcast([128, 8, 256]), op=ALU.mult),
                [bSTG[1], bLW], [bWKV])
            dma("sp", sm_stg[0], [(STG[0][:, 0:1024], wst_d[l])], [], [bSTG[0]])
            dve(lambda e: e.tensor_copy(out=WST[:].rearrange("p h i -> p (h i)"), in_=STG[0][:, 0:1024]),
                [bSTG[0]], [bWST])

        def load_layer_big(l):
            si = 0
            for kc in range(8):
                dma("sp", sm_stg[si], [(STG[si][:, 0:2048], win_d[l, kc * 128:(kc + 1) * 128, 0:2048]),
                                       (STG[si][:, 2048:2560], win_d[l, kc * 128:(kc + 1) * 128, OGB:OGB + 512])],
                    [], [bSTG[si]])
                if kc % 2 == 0:
                    act(lambda e, kc=kc, si=si: e.activation(out=WIN[:, kc, 0:2048], in_=STG[si][:, 0:2048], func=AF.Copy,
                                                             scale=PREG[:, kc:kc + 1]),
                        [bSTG[si], bLW], [bWIN[kc]])
                    act(lambda e, kc=kc, si=si: e.activation(out=WIN[:, kc, OGB:OGB + 512], in_=STG[si][:, 2048:2560],
                                                             func=AF.Copy, scale=PREG[:, kc:kc + 1]),
                        [bSTG[si], bLW], [bWIN[kc]])
                else:
                    dve(lambda e, kc=kc, si=si: e.tensor_scalar(out=WIN[:, kc, 0:2048], in0=STG[si][:, 0:2048],
                                                                scalar1=PREG[:, kc:kc + 1], scalar2=None,
                                                                op0=ALU.mult),
                        [bSTG[si], bLW], [bWIN[kc]])
                    dve(lambda e, kc=kc, si=si: e.tensor_scalar(out=WIN[:, kc, OGB:OGB + 512], in0=STG[si][:, 2048:2560],
                                                                scalar1=PREG[:, kc:kc + 1], scalar2=None,
                                                                op0=ALU.mult),
                        [bSTG[si], bLW], [bWIN[kc]])
                si ^= 1
                yield
            for kc in range(8):
                dma("sp", sm_stg[si], [(STG[si][:, 0:D], wout_d[l, kc * 128:(kc + 1) * 128, :])], [], [bSTG[si]])
                if kc % 2 == 0:
                    act(lambda e, kc=kc, si=si: e.activation(out=WOUT[:, kc, :], in_=STG[si][:, 0:D], func=AF.Copy),
                        [bSTG[si]], [bWOUT[kc]])
                else:
                    dve(lambda e, kc=kc, si=si: e.tensor_copy(out=WOUT[:, kc, :], in_=STG[si][:, 0:D]),
                        [bSTG[si]], [bWOUT[kc]])
                si ^= 1
                yield

        def block_front(src_d, s, b, slot):
            xt, xb, ht, st = XT[slot], XB[slot], HT[slot], ST[slot]
            dma("sp", sm_x[slot], [(xt[:], src_d[s, b * 128:(b + 1) * 128, :])], [bXD[s][b]], [bXT[slot]])
            act(lambda e: e.activation(out=xb[:], in_=xt[:], func=AF.Copy), [bXT[slot]], [bXB[slot]])
            act(lambda e: e.activation(out=JUNK[:], in_=xt[:], func=AF.Square, accum_out=st[:, C_SS:C_SS + 1]),
                [bXT[slot]], [bST[slot][0], bJUNK])
            rsqrt_chain(slot, C_SS, C_LN, C_RSTD, 1, 1.0 / D, [bST[slot][0]], bST[slot][1])

            def tr(e):
                ins = None
                for kc in range(8):
                    ins = e.transpose(out=PTR[:, kc * 128:(kc + 1) * 128], in_=xb[:, kc * 128:(kc + 1) * 128],
                                      identity=IDB[:])
                return ins
            pe(tr, [bXB[slot], bCONST], [bPTR])
            act(lambda e: e.activation(out=ht[:].rearrange("p k t -> p (k t)"), in_=PTR[:], func=AF.Copy), [bPTR], [bHT[slot]])

        def proj_part(co, n, slot, k0, k1, alt=False):
            ht = HT[slot]
            bank, bbank = (PTRF, bPTR) if alt else (PY, bY)

            def fn(e):
                ins = None
                for kc in range(k0, k1):
                    ins = e.matmul(bank[:, 0:n], lhsT=ht[:, kc, :], rhs=WIN[:, kc, co:co + n],
                                   start=(kc == 0), stop=(kc == 7))
                return ins
            pe(fn, [bHT[slot]] + ([bWKV] if co == OK_ else bWIN[k0:k1]), [bbank])

        def pview(t, g):
            return t[:, 0:512].rearrange("p (j g d) -> p j g d", j=4, g=2, d=64)[:, :, g, :]

        def nview(t, g):
            return t[:, g * 256:(g + 1) * 256].rearrange("p (j d) -> p j d", j=4)

        def rope(src, t1, t2, dst_view, H, b, bsrc, bt1, bt2, bdst, vw=None):
            cb = ROPC[:, b, :].unsqueeze(1).to_broadcast([128, H, 64])
            srs = ROPS[:, b, :].rearrange("p (a t i) -> p a t i", a=2, t=2, i=16)
            s3 = src[:, 0:H * 64].rearrange("p (h d) -> p h d", h=H)
            s5 = src[:, 0:H * 64].rearrange("p (h a t i) -> p h a t i", h=H, a=2, t=2, i=16)
            t15 = t1[:, 0:H * 64].rearrange("p (h d) -> p h d", h=H)
            t25 = t2[:, 0:H * 64].rearrange("p (h a t i) -> p h a t i", h=H, a=2, t=2, i=16)
            dve(lambda e: e.tensor_tensor(out=t15, in0=s3, in1=cb, op=ALU.mult), [bsrc, bCONST], [bt1])
            for t in range(2):
                sv = srs[:, :, t, :].unsqueeze(1).to_broadcast([128, H, 2, 16])
                dve(lambda e, t=t, sv=sv: e.tensor_tensor(out=t25[:, :, :, t, :], in0=s5[:, :, :, 1 - t, :], in1=sv,
                                                          op=ALU.mult), [bsrc, bCONST], [bt2])
            if vw is None:
                dve(lambda e: e.tensor_tensor(out=dst_view, in0=t1[:, 0:H * 64], in1=t2[:, 0:H * 64], op=ALU.add),
                    [bt1, bt2], [bdst])
            else:
                for g in range(2):
                    dve(lambda e, g=g: e.tensor_tensor(out=pview(dst_view, g), in0=nview(t1, g), in1=nview(t2, g),
                                                       op=ALU.add), [bt1, bt2], [bdst])

        P_SS, P_LN, P_RSTD = 44, 46, 48

        def phase1_seq(src_d, s, wgen=None):
            npair = NB // 2
            xsl = lambda i, t: (2 * i + t) % 3

            def loads(i):
                for t in range(2):
                    b = 2 * i + t
                    x3 = xsl(i, t)
                    dma("sp", sm_x[x3], [(XT[x3][:], src_d[s, b * 128:(b + 1) * 128, :])], [bXD[s][b]], [bXT[x3]])
            def do_pair(i):
                b0 = 2 * i
                sl = i % 2
                st = ST[sl]
                bs = bST[sl]
                for t in range(2):
                    x3 = xsl(i, t)
                    dve(lambda e, t=t, x3=x3: e.tensor_copy(out=XB[t][:], in_=XT[x3][:]), [bXT[x3]], [bXB[t]])
                    act(lambda e, t=t, x3=x3: e.activation(out=JUNK[:], in_=XT[x3][:], func=AF.Square,
                                                           accum_out=st[:, P_SS + t:P_SS + t + 1]),
                        [bXT[x3]], [bs[0], bJUNK])
                if i + 1 < npair:
                    loads(i + 1)
                rsqrt_chain(sl, P_SS, P_LN, P_RSTD, 2, 1.0 / D, [bs[0]], bs[1])
                pool(lambda e: e.tensor_copy(out=RALL[:, b0:b0 + 2], in_=st[:, P_RSTD:P_RSTD + 2]), [bs[1]],
                     [bRALL[b0], bRALL[b0 + 1]])
                for t in range(2):
                    def tr(e, t=t):
                        ins = None
                        for kc in range(8):
                            ins = e.transpose(out=PTR[:, kc * 128:(kc + 1) * 128], in_=XB[t][:, kc * 128:(kc + 1) * 128],
                                              identity=IDB[:])
                        return ins
                    pe(tr, [bXB[t], bCONST], [bPTR])
                    act(lambda e, t=t: e.activation(out=HT[t][:].rearrange("p k t -> p (k t)"), in_=PTR[:], func=AF.Copy),
                        [bPTR], [bHT[t]])
                    dma("sp", sm_hts[t], [(htd[b0 + t], HT[t][:].rearrange("p k t -> p (k t)"))], [bHT[t]], [bHTD[b0 + t]])
                for t in range(2):
                    def pj(e, t=t):
                        ins = None
                        for kc in range(8):
                            ins = e.matmul(PY[:, t * 256:(t + 1) * 256], lhsT=HT[t][:, kc, :], rhs=WIN[:, kc, OK_:OK_ + 256],
                                           start=(kc == 0), stop=(kc == 7))
                        return ins
                    pe(pj, [bHT[t], bWKV], [bY])
                rs2 = st[:, P_RSTD:P_RSTD + 2]
                pyv = PY[:].rearrange("p (t c) -> p t c", t=2)
                dve(lambda e: e.tensor_tensor(out=KS2[:].rearrange("p (t c) -> p t c", t=2), in0=pyv[:, :, 0:128],
                                              in1=rs2.unsqueeze(2).to_broadcast([128, 2, 128]), op=ALU.mult),
                    [bY, bs[1]], [bKS2])
                for t in range(2):
                    rst = st[:, P_RSTD + t:P_RSTD + t + 1]
                    act(lambda e, t=t, rst=rst: e.activation(out=VX[:, b0 + t, 0:64], in_=PY[:, t * 256 + 128:t * 256 + 192],
                                                             func=AF.Copy, scale=rst), [bY, bs[1]], [bVX[b0 + t]])
                    act(lambda e, t=t, rst=rst: e.activation(out=VX[:, b0 + t, 128:192], in_=PY[:, t * 256 + 192:t * 256 + 256],
                                                             func=AF.Copy, scale=rst), [bY, bs[1]], [bVX[b0 + t]])
                dve(lambda e: e.tensor_tensor(out=KT1w[:], in0=KS2[:], in1=KS2[:], op=ALU.mult), [bKS2], [bKT1w])
                dve(lambda e: e.tensor_reduce(out=st[:, C_SSQ:C_SSQ + 4], in_=KT1w[:].rearrange("p (h d) -> p h d", h=4),
                                              axis=AX.X, op=ALU.add), [bKT1w], [bs[4]])
                rsqrt_chain(sl, C_SSQ, C_LNQ, C_RQ, 4, 1.0 / 64, [bs[4]], bs[5])
                k4 = KS2[:].rearrange("p (h d) -> p h d", h=4)
                dve(lambda e: e.tensor_tensor(out=k4, in0=k4, in1=st[:, C_RQ:C_RQ + 4].unsqueeze(2).to_broadcast([128, 4, 64]),
                                              op=ALU.mult), [bKS2, bs[5]], [bKS2])
                dve(lambda e: e.tensor_tensor(out=k4, in0=k4, in1=KG[:].unsqueeze(1).to_broadcast([128, 4, 64]),
                                              op=ALU.mult), [bKS2, bLW], [bKS2])
                for t in range(2):
                    dve(lambda e, t=t: e.tensor_tensor(out=KT1w[:, t * 128:(t + 1) * 128].rearrange("p (h d) -> p h d", h=2),
                                                       in0=KS2[:, t * 128:(t + 1) * 128].rearrange("p (h d) -> p h d", h=2),
                                                       in1=ROPC[:, b0 + t, :].unsqueeze(1).to_broadcast([128, 2, 64]),
                                                       op=ALU.mult), [bKS2, bCONST], [bKT1w])
                for t in range(2):
                    srs = ROPS[:, b0 + t, :].rearrange("p (a u i) -> p a u i", a=2, u=2, i=16)
                    s5 = KS2[:, t * 128:(t + 1) * 128].rearrange("p (h a u i) -> p h a u i", h=2, a=2, u=2, i=16)
                    t25 = KT2w[:, t * 128:(t + 1) * 128].rearrange("p (h a u i) -> p h a u i", h=2, a=2, u=2, i=16)
                    for u in range(2):
                        sv = srs[:, :, u, :].unsqueeze(1).to_broadcast([128, 2, 2, 16])
                        dve(lambda e, u=u, sv=sv, s5=s5, t25=t25: e.tensor_tensor(out=t25[:, :, :, u, :],
                                                                                  in0=s5[:, :, :, 1 - u, :], in1=sv,
                                                                                  op=ALU.mult),
                            [bKS2, bCONST], [bKT2w])
                dve(lambda e: e.tensor_tensor(out=KRw[:], in0=KT1w[:], in1=KT2w[:], op=ALU.add), [bKT1w, bKT2w], [bKRw])

                def trk(e):
                    ins = None
                    for t in range(2):
                        ins = e.transpose(out=PTR[:, t * 128:(t + 1) * 128], in_=KRw[:, t * 128:(t + 1) * 128],
                                          identity=IDB[:])
                    return ins
                pe(trk, [bKRw, bCONST], [bPTR])
                dve(lambda e: e.tensor_copy(out=KT[:, b0 * 128:(b0 + 2) * 128], in_=PTR[:, 0:256]), [bPTR],
                    [bKT[b0], bKT[b0 + 1]])
                if wgen is not None:
                    next(wgen, None)
            loads(0)
            for i in range(npair):
                do_pair(i)
            if wgen is not None:
                for _ in wgen:
                    pass

        def tr4(src):
            def fn(e):
                ins = None
                for c in range(4):
                    ins = e.transpose(out=PTR[:, c * 128:(c + 1) * 128], in_=src[:, c * 128:(c + 1) * 128],
                                      identity=IDB[:])
                return ins
            return fn

        def tanh_half(src, bsrc):
            act(lambda e: e.activation(out=EG[:], in_=src[:], func=AF.Tanh, scale=0.5), [bsrc], [bb["EG"]])

        def planA(src_d, s, b, slot, x3, with_f0=True, only_f0=False):
            st = ST[slot]
            bs = bST[slot]
            yt = YT[slot]
            rstd = RALL[:, b:b + 1]
            plan = {}

            def at(c, fn):
                plan.setdefault(c, []).append(fn)

            def f0a():
                dma("sp", sm_x[x3], [(XT[x3][:], src_d[s, b * 128:(b + 1) * 128, :])], [bXD[s][b]], [bXT[x3]])
                dma("sp", sm_htl[slot], [(HT[slot][:].rearrange("p k t -> p (k t)"), htd[b])], [bHTD[b]], [bHT[slot]])

            if only_f0:
                return {27: [f0a]}
            if with_f0:
                at(0, f0a)

            def group(c, co, dst, nm, alt=False):
                bank, bbank = (PTRF, bPTR) if alt else (PY, bY)
                for i in range(4):
                    if i < 3:
                        at(c + i, lambda i=i: proj_part(co, 512, slot, 2 * i, 2 * i + 2, alt))
                    else:
                        def last():
                            proj_part(co, 512, slot, 6, 8, alt)
                            dve(lambda e: e.tensor_scalar(out=dst[:], in0=bank[:, 0:512], scalar1=rstd, scalar2=None,
                                                          op0=ALU.mult), [bbank, bRALL[b]], [bb[nm]])
                        at(c + 3, last)
            group(3, OVA, VA, "VA")
            group(12, OQ, QS, "QS")
            group(21, OGA, GA, "GA", alt=True)
            group(24, OU, UU, "UU")
            group(28, OGB, GB, "GB", alt=True)

            def ln():
                dve(lambda e: e.bn_stats(out=st[:, C_BN:C_BN + 6], in_=VA[:]), [bb["VA"]], [bs[6]])
                dve(lambda e: e.bn_aggr(out=st[:, C_MV:C_MV + 2], in_=st[:, C_BN:C_BN + 6]), [bs[6]], [bs[7]])
                rsqrt_chain(slot, C_MV + 1, C_LNV, C_RSTDV, 1, 1.0, [bs[7]], bs[8])
                dve(lambda e: e.scalar_tensor_tensor(out=VA[:], in0=VA[:], scalar=st[:, C_MV:C_MV + 1], in1=LNG[:],
                                                     op0=ALU.subtract, op1=ALU.mult), [bb["VA"], bs[7], bLW], [bb["VA"]])
            at(7, ln)
            at(9, lambda: dve(lambda e: e.scalar_tensor_tensor(out=VN[:], in0=VA[:], scalar=st[:, C_RSTDV:C_RSTDV + 1],
                                                                in1=LNB[:], op0=ALU.mult, op1=ALU.add),
                               [bb["VA"], bs[8], bLW], [bb["VN"]]))

            def q1():
                dve(lambda e: e.tensor_tensor(out=T1[:], in0=QS[:], in1=QS[:], op=ALU.mult), [bb["QS"]], [bb["T1"]])
                dve(lambda e: e.tensor_reduce(out=st[:, C_SSQ:C_SSQ + 8], in_=T1[:].rearrange("p (h d) -> p h d", h=8),
                                              axis=AX.X, op=ALU.add), [bb["T1"]], [bs[4]])
                rsqrt_chain(slot, C_SSQ, C_LNQ, C_RQ, 8, 1.0 / 64, [bs[4]], bs[5])
            at(16, q1)

            def q2():
                dve(lambda e: e.tensor_tensor(out=QS[:].rearrange("p (h d) -> p h d", h=8),
                                              in0=QS[:].rearrange("p (h d) -> p h d", h=8),
                                              in1=st[:, C_RQ:C_RQ + 8].unsqueeze(2).to_broadcast([128, 8, 64]),
                                              op=ALU.mult), [bb["QS"], bs[5]], [bb["QS"]])
                dve(lambda e: e.tensor_tensor(out=QS[:].rearrange("p (h d) -> p h d", h=8),
                                              in0=QS[:].rearrange("p (h d) -> p h d", h=8),
                                              in1=QG[:].unsqueeze(1).to_broadcast([128, 8, 64]),
                                              op=ALU.mult), [bb["QS"], bLW], [bb["QS"]])
            at(18, q2)
            at(21, lambda: rope(QS, T1, T2, QP, 8, b, bb["QS"], bb["T1"], bb["T2"], bb["QP"], vw=True))

            def zz():
                def zmm(e):
                    ins = None
                    for h in range(8):
                        ins = e.matmul(PY[:, h * 64:(h + 1) * 64], lhsT=WST[:, h, :], rhs=VN[:, h * 64:(h + 1) * 64],
                                       start=True, stop=True)
                    return ins
                pe(zmm, [bb["VN"], bWST], [bY])
                dve(lambda e: e.tensor_tensor(out=ZB[:].rearrange("p (h d) -> p h d", h=8),
                                              in0=PY[:].rearrange("p (h d) -> p h d", h=8),
                                              in1=SBT[:].unsqueeze(2).to_broadcast([128, 8, 64]), op=ALU.add),
                    [bY, bLW], [bb["ZB"]])
            at(20, zz)

            def trq():
                pe(tr4(QP), [bb["QP"], bCONST], [bPTR])
                dve(lambda e: e.tensor_copy(out=QT[slot][:], in_=PTR[:, 0:512]), [bPTR], [bQT[slot]])
            at(26, trq)
            at(26, lambda: tanh_half(GA, bb["GA"]))

            def gate_a():
                dve(lambda e: e.scalar_tensor_tensor(out=GA[:], in0=EG[:], scalar=1.0, in1=GA[:], op0=ALU.add,
                                                     op1=ALU.mult), [bb["GA"], bb["EG"]], [bb["GA"]])
                dve(lambda e: e.scalar_tensor_tensor(out=UU[:], in0=UU[:], scalar=0.5, in1=GA[:], op0=ALU.mult,
                                                     op1=ALU.mult), [bb["UU"], bb["GA"]], [bb["UU"]])
                dve(lambda e: e.tensor_tensor(out=YA[:], in0=UU[:], in1=ZB[:], op=ALU.mult), [bb["UU"], bb["ZB"]],
                    [bb["YA"]])
            at(29, gate_a)
            at(33, lambda: tanh_half(GB, bb["GB"]))

            def gate_b():
                for g in range(2):
                    dve(lambda e, g=g: e.scalar_tensor_tensor(out=pview(GBS, g), in0=nview(EG, g), scalar=1.0,
                                                              in1=nview(GB, g), op0=ALU.add, op1=ALU.mult),
                        [bb["GB"], bb["EG"]], [bb["GBS"]])
            at(36, gate_b)

            def tra():
                pe(tr4(YA), [bb["YA"], bCONST], [bPTR])
                dve(lambda e: e.tensor_copy(out=yt[:, 0:4, :].rearrange("p k t -> p (k t)"), in_=PTR[:, 0:512]),
                    [bPTR], [bYT[slot][0]])
            at(35, tra)

            def trg():
                pe(tr4(GBS), [bb["GBS"], bCONST], [bPTR])
                dve(lambda e: e.tensor_copy(out=GT[slot][:], in_=PTR[:, 0:512]), [bPTR], [bGT[slot]])
            at(39, trg)
            return plan

        def planC(s, b, slot, x3):
            st = ST[slot]
            bs = bST[slot]
            yt = YT[slot]
            plan = {}

            def at(c, fn):
                plan.setdefault(c, []).append(fn)

            def op_part(n, k0, k1, evac):
                def fn():
                    def oproj(e):
                        ins = None
                        for kc in range(k0, k1):
                            ins = e.matmul(PTRF[:, 0:512], lhsT=yt[:, kc, :], rhs=WOUT[:, kc, n * 512:(n + 1) * 512],
                                           start=(kc == 0), stop=(kc == 7))
                        return ins
                    pe(oproj, [bYT[slot][0], bYT[slot][1]] + bWOUT[k0:k1], [bPTR])
                    if evac:
                        dve(lambda e: e.tensor_copy(out=TMP[:, n * 512:(n + 1) * 512], in_=PTRF[:, 0:512]), [bPTR],
                            [bb["TMP"]])
                return fn
            for n, c0 in ((0, 10), (1, 16)):
                for i in range(4):
                    at(c0 + i, op_part(n, 2 * i, 2 * i + 2, i == 3))
            xs = 0
            at(21, lambda: act(lambda e: e.activation(out=JUNK[:], in_=TMP[:], func=AF.Square,
                                                      accum_out=st[:, C_SS2:C_SS2 + 1]), [bb["TMP"]], [bs[11], bJUNK]))
            at(23, lambda: rsqrt_chain(slot, C_SS2, C_LN2, C_RSTD2, 1, 1.0 / D, [bs[11]], bs[12]))

            def fin():
                dve(lambda e: e.scalar_tensor_tensor(out=XO[xs][:], in0=TMP[:], scalar=st[:, C_RSTD2:C_RSTD2 + 1],
                                                     in1=POSTG[:], op0=ALU.mult, op1=ALU.mult),
                    [bb["TMP"], bs[12], bLW], [bXO[xs]])
                dve(lambda e: e.tensor_tensor(out=XO[xs][:], in0=XO[xs][:], in1=XT[x3][:], op=ALU.add),
                    [bXO[xs], bXT[x3]], [bXO[xs]])
            at(25, fin)
            at(26, lambda: dma("pool", sm_xo[xs], [(out_d[s, b * 128:(b + 1) * 128, :], XO[xs][:])], [bXO[xs]],
                               [bXD[s][b]]))
            return plan

        def run_plan(plans, c):
            for p in plans:
                for fn in p.get(c, ()):
                    fn()

        def attention(b, slot, plans):
            qt = QT[slot]

            def qk(c):
                par = c % 2

                def fn(e):
                    ins = None
                    for g in range(2):
                        ins = e.matmul(SPS[:, par * 1024 + g * 512: par * 1024 + (g + 1) * 512],
                                       lhsT=KT[g * 64:(g + 1) * 64, c * 128:(c + 1) * 128],
                                       rhs=qt[g * 64:(g + 1) * 64, :], start=True, stop=True,
                                       tile_position=(g * 64, 0))
                    return ins
                pe(fn, [bKT[c], bQT[slot]], [bS[par]])
            qk(0)
            pend_pv = None
            for c in range(NB):
                par = c % 2
                k = c % NE
                act(lambda e, k=k, par=par: e.activation(out=ET[k][:], in_=SPS[:, par * 1024:(par + 1) * 1024],
                                                         func=AF.Exp, scale=0.125), [bS[par]], [bET[k]])
                if c + 1 < NB:
                    qk(c + 1)

                def pv(e, c=c, k=k):
                    ins = None
                    for g in range(2):
                        ins = e.matmul(PO[g][:], lhsT=VX[:, c, g * 64:g * 64 + 128], rhs=ET[k][:, g * 512:(g + 1) * 512],
                                       start=(c == 0), stop=(c == NB - 1))
                    return ins
                run_plan(plans, c)
                if pend_pv is not None:
                    pe(pend_pv[0], pend_pv[1], [bPO[0], bPO[1]])
                pend_pv = (pv, [bVX[c], bET[k]])
            pe(pend_pv[0], pend_pv[1], [bPO[0], bPO[1]])

        def stageC0(slot):
            dve(lambda e: e.tensor_copy(out=A0[:], in_=PO[0][:]), [bPO[0]], [bb["A0"]])
            dve(lambda e: e.tensor_copy(out=A1[:], in_=PO[1][:]), [bPO[1]], [bb["A1"]])

        def planC0(slot):
            yt = YT[slot]
            plan = {}

            def at(c, fn):
                plan.setdefault(c, []).append(fn)

            def asm():
                dve(lambda e: e.tensor_copy(out=RD[0:64, :], in_=A0[64:128, :]), [bb["A0"]], [bb["RD"]])
                dve(lambda e: e.tensor_copy(out=RD[64:128, :], in_=A1[0:64, :]), [bb["A1"]], [bb["RD"]])
            at(3, asm)

            def rec(j):
                return lambda: dve(lambda e: e.reciprocal(out=RD[:, j * 128:(j + 1) * 128], in_=RD[:, j * 128:(j + 1) * 128]),
                                   [bb["RD"]], [bb["RD"]])
            at(4, rec(0))
            at(4, rec(1))
            at(5, rec(2))
            at(5, rec(3))

            def fin0():
                dve(lambda e: e.tensor_tensor(out=TT[0:64, :], in0=A0[0:64, :], in1=RD[0:64, :], op=ALU.mult),
                    [bb["A0"], bb["RD"]], [bb["TT"]])
                dve(lambda e: e.tensor_tensor(out=TT[64:128, :], in0=A1[64:128, :], in1=RD[64:128, :], op=ALU.mult),
                    [bb["A1"], bb["RD"]], [bb["TT"]])
                dve(lambda e: e.scalar_tensor_tensor(out=yt[:, 4:8, :].rearrange("p k t -> p (k t)"), in0=TT[:],
                                                     scalar=0.5, in1=GT[slot][:], op0=ALU.mult, op1=ALU.mult),
                    [bb["TT"], bGT[slot]], [bYT[slot][1]])
            at(6, fin0)
            return plan

        def split_plan(p):
            now = {c: f for c, f in p.items() if c < NB}
            later = {c - NB: f for c, f in p.items() if c >= NB}
            return now, later

        def phase2_seq(src_d, s):
            slots = []
            for b in range(NB):
                slots.append(blk_counter[0] % 2)
                blk_counter[0] += 1
            p0 = planA(src_d, s, 0, slots[0], 0)
            for c in sorted(p0):
                run_plan([p0], c)
            prevC = None
            carry = None
            for b in range(NB):
                plans = []
                if prevC is not None:
                    plans.append(prevC0)
                    plans.append(prevC)
                if carry:
                    plans.append(carry)
                carry = None
                if b + 1 < NB:
                    now, carry = split_plan(planA(src_d, s, b + 1, slots[b + 1], (b + 1) % 3, with_f0=(b == 0)))
                    plans.append(now)
                if b + 2 < NB:
                    plans.append(planA(src_d, s, b + 2, slots[b + 2], (b + 2) % 3, only_f0=True))
                attention(b, slots[b], plans)
                stageC0(slots[b])
                prevC0 = planC0(slots[b])
                prevC = planC(s, b, slots[b], b % 3)
            for c in sorted(set(prevC) | set(prevC0)):
                run_plan([prevC0, prevC], c)

        import os
        stage = int(os.environ.get("K_STAGE", "99"))
        for l in range(L):
            src = x_d if l == 0 else out_d
            load_layer_small(l)
            wgen = load_layer_big(l)
            if stage == 0:
                for _ in wgen:
                    pass
                continue
            for s in range(NSEQ):
                phase1_seq(src, s, wgen if s == 0 else None)
                if stage == 1:
                    break
                phase2_seq(src, s)
                if stage == 2:
                    break

        P.finalize(eng_sems)
        last_xo = {}
        for o in P.q["pool"]:
            if o.dma_sem is not None:
                last_xo[id(o.dma_sem)] = (o.dma_sem, o.val)
        finals = list(last_xo.values())

        block = es.enter_context(nc.Block())

        @block.sync
        def _(e):
            P.emit("sp", e, eng_sems)

        @block.scalar
        def _(e):
            P.emit("act", e, eng_sems)

        @block.vector
        def _(e):
            P.emit("dve", e, eng_sems)

        @block.tensor
        def _(e):
            P.emit("pe", e, eng_sems)

        @block.gpsimd
        def _(e):
            P.emit("pool", e, eng_sems, final_waits=finals)
    return nc


def _rope_tables():
    t = np.arange(S)
    row = (t // 64).astype(np.float32)
    col = (t % 64).astype(np.float32)
    inv = (np.float32(10000.0) ** (-np.arange(0, 32, 2, dtype=np.float32) / np.float32(32))).astype(np.float32)
    ar = (row[:, None] * inv[None, :]).astype(np.float32)
    ac = (col[:, None] * inv[None, :]).astype(np.float32)
    cr, sr, cc, sc = np.cos(ar), np.sin(ar), np.cos(ac), np.sin(ac)
    c = np.concatenate([cr, cr, cc, cc], axis=1).astype(np.float32)
    s = np.concatenate([-sr, sr, -sc, sc], axis=1).astype(np.float32)
    c = c.reshape(NB, 128, 64).transpose(1, 0, 2).reshape(128, NB * 64)
    s = s.reshape(NB, 128, 64).transpose(1, 0, 2).reshape(128, NB * 64)
    return np.ascontiguousarray(c), np.ascontiguousarray(s)


def _wout_perm():
    idx = np.zeros(1024, dtype=np.int64)
    for c in range(8):
        for p in range(128):
            if c < 4:
                idx[c * 128 + p] = c * 128 + p
            else:
                j = c - 4
                idx[c * 128 + p] = 512 + j * 64 + p if p < 64 else 512 + (4 + j) * 64 + (p - 64)
    return idx


_NC_CACHE = {}


def _get_nc(n_layers):
    if n_layers not in _NC_CACHE:
        _NC_CACHE[n_layers] = build_nc(n_layers)
    return _NC_CACHE[n_layers]


def kernel(x, w_in, w_out, pre_g, post_g, ln_a_g, ln_a_b, spatial_w, spatial_b, q_norm_g, k_norm_g):
    f = lambda a: np.ascontiguousarray(np.asarray(a, dtype=np.float32))
    x = f(x)
    w_in = f(w_in)
    w_out_p = f(np.asarray(w_out)[:, _wout_perm(), :])
    pre_g_t = f(np.asarray(pre_g).reshape(DEPTH, 8, 128).transpose(0, 2, 1))
    post_g = f(post_g)
    ln_g = f(ln_a_g)
    ln_b = f(ln_a_b)
    ws_t = f(np.asarray(spatial_w).transpose(0, 3, 1, 2).reshape(DEPTH, 128, 1024))
    sb_t = f(np.asarray(spatial_b).transpose(0, 2, 1))
    qg = f(q_norm_g)
    kg = f(k_norm_g)
    rc, rs = _rope_tables()
    ident = np.eye(128, dtype=np.float32)
    n_cores = 8
    nl = N_LAYERS_PER_LAUNCH
    nc = _get_nc(nl)
    cur = [np.ascontiguousarray(x[NSEQ * c:NSEQ * (c + 1)]) for c in range(n_cores)]
    for l0 in range(0, DEPTH, nl):
        sl = slice(l0, l0 + nl)
        in_maps = []
        for c in range(n_cores):
            in_maps.append({
                "x": cur[c], "w_in": w_in[sl], "w_out_p": w_out_p[sl], "pre_g_t": pre_g_t[sl], "post_g": post_g[sl],
                "ln_g": ln_g[sl], "ln_b": ln_b[sl], "ws_t": ws_t[sl], "sb_t": sb_t[sl], "qg": qg[sl], "kg": kg[sl],
                "rope_c": rc, "rope_s": rs, "ident": ident,
            })
        res = run_bass_kernel_spmd(nc, in_maps, core_ids=list(range(n_cores)))
        cur = [np.ascontiguousarray(res.results[c]["out"]) for c in range(n_cores)]
    return np.concatenate(cur, axis=0).astype(np.float32)
```
